# Optimizing a Trainium2 kernel written in Bass

```python
import math
import jax, jax.numpy as jnp
from jax import lax
import numpy as np

D_MODEL = 1024
BATCH = 16
SEQ = 2048
DEPTH = 2

N_META = 16
N_EVEN = (DEPTH + 1) // 2
N_ODD = DEPTH // 2

LRU_WIDTH = D_MODEL // 2
LRU_HEADS = 4
LRU_HEAD_DIM = LRU_WIDTH // LRU_HEADS
CONV_WIDTH = 4
LRU_C = 8.0

MLA_HEADS = 8
MLA_NOPE = 64
MLA_ROPE = 32
MLA_V = 64
MLA_Q_RANK = D_MODEL // 4
MLA_KV_RANK = D_MODEL // 8
ATTN_BLOCK = 128

EVEN_IN = 2 * LRU_WIDTH + MLA_Q_RANK + MLA_KV_RANK + MLA_ROPE
EVEN_MIX = LRU_WIDTH + MLA_HEADS * MLA_V

RET_HEADS = 4
RET_QK_DIM = D_MODEL // RET_HEADS
RET_V_DIM = 2 * RET_QK_DIM
RET_CHUNK = 128
RET_IN = 2 * RET_HEADS * RET_QK_DIM + 2 * RET_HEADS * RET_V_DIM
RET_MIX = RET_HEADS * RET_V_DIM

D_FF = 4 * D_MODEL
ROPE_BASE = 10000.0
DN_ALPHA = (2 * DEPTH) ** 0.25
DN_BETA = (8 * DEPTH) ** -0.25
EPS = 1e-5
NEG_INF = -1e30

kernel_name = 'hybrid_rglru_mla_retention_deepnorm'


def _layernorm(x, g, b):
    xf = x.astype(jnp.float32)
    mu = jnp.mean(xf, axis=-1, keepdims=True)
    xc = xf - mu
    var = jnp.mean(jnp.square(xc), axis=-1, keepdims=True)
    return (xc * lax.rsqrt(var + EPS) * g + b).astype(x.dtype)


def _rmsnorm(x, g):
    xf = x.astype(jnp.float32)
    y = xf * lax.rsqrt(jnp.mean(jnp.square(xf), axis=-1, keepdims=True) + EPS)
    return (y * g).astype(x.dtype)


def _rope(x, pos):
    half = x.shape[-1] // 2
    inv = ROPE_BASE ** (-jnp.arange(half, dtype=jnp.float32) / half)
    ang = pos.astype(jnp.float32)[:, None] * inv[None, :]
    cos = jnp.cos(ang)[None, :, None, :].astype(x.dtype)
    sin = jnp.sin(ang)[None, :, None, :].astype(x.dtype)
    x1, x2 = x[..., :half], x[..., half:]
    return jnp.concatenate([x1 * cos - x2 * sin, x1 * sin + x2 * cos], axis=-1)


def _lru_combine(c1, c2):
    a1, b1 = c1
    a2, b2 = c2
    return a1 * a2, a2 * b1 + b2


def _rglru_group(p_gate, p_rec, conv_w, conv_b, w_rg_a, b_rg_a, w_rg_x, b_rg_x, lru_lambda):
    B, T, _ = p_rec.shape
    xc = lax.conv_general_dilated(
        p_rec, conv_w[:, None, :], window_strides=(1,), padding=[(CONV_WIDTH - 1, 0)],
        dimension_numbers=('NWC', 'WIO', 'NWC'), feature_group_count=LRU_WIDTH) + conv_b
    xh = xc.reshape(B, T, LRU_HEADS, LRU_HEAD_DIM)
    r = jax.nn.sigmoid(jnp.einsum('bthi,hij->bthj', xh, w_rg_a).reshape(B, T, LRU_WIDTH) + b_rg_a)
    i = jax.nn.sigmoid(jnp.einsum('bthi,hij->bthj', xh, w_rg_x).reshape(B, T, LRU_WIDTH) + b_rg_x)
    log_a = (-LRU_C * r * jax.nn.softplus(-lru_lambda)).astype(jnp.float32)
    a = jnp.exp(log_a)
    mult = jnp.sqrt(-jnp.expm1(2.0 * log_a))
    b = mult * (i * xc).astype(jnp.float32)
    _, h = lax.associative_scan(_lru_combine, (a, b), axis=1)
    return h.astype(p_rec.dtype) * jax.nn.gelu(p_gate)


def _attend(qb, qpos, k, v, kpos):
    scale = qb.shape[-1] ** -0.5
    s = jnp.einsum('bqhd,bkhd->bhqk', qb, k).astype(jnp.float32) * scale
    mask = kpos[None, :] <= qpos[:, None]
    s = jnp.where(mask[None, None], s, NEG_INF)
    p = jax.nn.softmax(s, axis=-1).astype(v.dtype)
    return jnp.einsum('bhqk,bkhd->bqhd', p, v)


def _causal_attention(q, k, v, pos):
    B, T, H, d = q.shape
    dv = v.shape[-1]
    out_meta = _attend(q[:, :N_META], pos[:N_META], k[:, :N_META], v[:, :N_META], pos[:N_META])
    nb = (T - N_META) // ATTN_BLOCK
    qr = q[:, N_META:].reshape(B, nb, ATTN_BLOCK, H, d).swapaxes(0, 1)
    pr = pos[N_META:].reshape(nb, ATTN_BLOCK)
    out_r = lax.map(lambda a: _attend(a[0], a[1], k, v, pos), (qr, pr))
    out_r = out_r.swapaxes(0, 1).reshape(B, T - N_META, H, dv)
    return jnp.concatenate([out_meta, out_r], axis=1)


def _mla_group(p_q, p_kv, p_kpe, pos, q_norm_g, w_uq, kv_norm_g, w_ukv):
    B, T, _ = p_q.shape
    q = (_rmsnorm(p_q, q_norm_g) @ w_uq).reshape(B, T, MLA_HEADS, MLA_NOPE + MLA_ROPE)
    q_nope, q_pe = q[..., :MLA_NOPE], _rope(q[..., MLA_NOPE:], pos)
    kv = (_rmsnorm(p_kv, kv_norm_g) @ w_ukv).reshape(B, T, MLA_HEADS, MLA_NOPE + MLA_V)
    k_nope, v = kv[..., :MLA_NOPE], kv[..., MLA_NOPE:]
    k_pe = _rope(p_kpe[:, :, None, :], pos)
    q = jnp.concatenate([q_nope, q_pe], axis=-1)
    k = jnp.concatenate([k_nope, jnp.broadcast_to(k_pe, (B, T, MLA_HEADS, MLA_ROPE))], axis=-1)
    o = _causal_attention(q, k, v, pos)
    return o.reshape(B, T, MLA_HEADS * MLA_V)


def _even_mixer(x, pos, w_in, conv_w, conv_b, w_rg_a, b_rg_a, w_rg_x, b_rg_x, lru_lambda,
                q_norm_g, w_uq, kv_norm_g, w_ukv, w_out):
    p = x @ w_in
    cuts = [LRU_WIDTH, 2 * LRU_WIDTH, 2 * LRU_WIDTH + MLA_Q_RANK,
            2 * LRU_WIDTH + MLA_Q_RANK + MLA_KV_RANK]
    p_gate, p_rec, p_q, p_kv, p_kpe = jnp.split(p, cuts, axis=-1)
    y_rec = _rglru_group(p_gate, p_rec, conv_w, conv_b, w_rg_a, b_rg_a, w_rg_x, b_rg_x, lru_lambda)
    y_att = _mla_group(p_q, p_kv, p_kpe, pos, q_norm_g, w_uq, kv_norm_g, w_ukv)
    return jnp.concatenate([y_rec, y_att], axis=-1) @ w_out


def _retention_chunk(q, k, v, s_prev, log_gamma):
    dt = q.dtype
    c = q.shape[2]
    idx = jnp.arange(c, dtype=jnp.float32)
    diff = idx[:, None] - idx[None, :]
    decay = jnp.where(diff >= 0, jnp.exp(log_gamma[:, None, None] * jnp.maximum(diff, 0.0)), 0.0).astype(dt)
    q_decay = jnp.exp(log_gamma[:, None] * (idx + 1.0))[None, :, :, None].astype(dt)
    k_decay = jnp.exp(log_gamma[:, None] * (c - 1.0 - idx))[None, :, :, None].astype(dt)
    chunk_decay = jnp.exp(log_gamma * c)[None, :, None, None].astype(dt)
    scores = jnp.einsum('bhid,bhjd->bhij', q, k) * decay
    o = jnp.einsum('bhij,bhjv->bhiv', scores, v) + q_decay * jnp.einsum('bhid,bhdv->bhiv', q, s_prev)
    s_new = chunk_decay * s_prev + jnp.einsum('bhjd,bhjv->bhdv', k * k_decay, v)
    return o, s_new


def _odd_mixer(x, pos, w_in, w_out):
    B, T, _ = x.shape
    qk = RET_HEADS * RET_QK_DIM
    p = x @ w_in
    q, k, v, g = jnp.split(p, [qk, 2 * qk, 2 * qk + RET_MIX], axis=-1)
    q = _rope(q.reshape(B, T, RET_HEADS, RET_QK_DIM), pos)
    k = _rope(k.reshape(B, T, RET_HEADS, RET_QK_DIM), pos) * (RET_QK_DIM ** -0.5)
    v = v.reshape(B, T, RET_HEADS, RET_V_DIM)
    q, k, v = (t.transpose(0, 2, 1, 3) for t in (q, k, v))
    log_gamma = jnp.log(1.0 - 2.0 ** (-5.0 - jnp.arange(RET_HEADS, dtype=jnp.float32)))
    s0 = jnp.zeros((B, RET_HEADS, RET_QK_DIM, RET_V_DIM), dtype=q.dtype)
    o_meta, s = _retention_chunk(q[:, :, :N_META], k[:, :, :N_META], v[:, :, :N_META], s0, log_gamma)
    nc = (T - N_META) // RET_CHUNK

    def to_chunks(t):
        return t[:, :, N_META:].reshape(B, RET_HEADS, nc, RET_CHUNK, t.shape[-1]).transpose(2, 0, 1, 3, 4)

    def body(state, qkv):
        qc, kc, vc = qkv
        o, state = _retention_chunk(qc, kc, vc, state, log_gamma)
        return state, o

    _, o_r = lax.scan(body, s, (to_chunks(q), to_chunks(k), to_chunks(v)))
    o_r = o_r.transpose(1, 2, 0, 3, 4).reshape(B, RET_HEADS, T - N_META, RET_V_DIM)
    o = jnp.concatenate([o_meta, o_r], axis=2)
    of = o.astype(jnp.float32)
    o = (of * lax.rsqrt(jnp.mean(jnp.square(of), axis=-1, keepdims=True) + EPS)).astype(x.dtype)
    y = o.transpose(0, 2, 1, 3).reshape(B, T, RET_MIX)
    return (jax.nn.silu(g) * y) @ w_out


def setup_inputs(seed: int = 0) -> dict:
    key = jax.random.key(seed)
    ks = iter(jax.random.split(key, 40))
    f32 = jnp.float32

    def nrm(shape, scale):
        return jax.random.normal(next(ks), shape, f32) * scale

    u = jax.random.uniform(next(ks), (N_EVEN, LRU_WIDTH), f32, minval=0.9, maxval=0.999)
    a_base = u ** (1.0 / LRU_C)
    lru_lambda = jnp.log(a_base) - jnp.log1p(-a_base)
    return {
        'x': nrm((BATCH, SEQ, D_MODEL), 1.0),
        'meta_tokens': nrm((N_META, D_MODEL), 1.0),
        'ev_w_in': nrm((N_EVEN, D_MODEL, EVEN_IN), D_MODEL ** -0.5),
        'ev_conv_w': nrm((N_EVEN, CONV_WIDTH, LRU_WIDTH), CONV_WIDTH ** -0.5),
        'ev_conv_b': nrm((N_EVEN, LRU_WIDTH), 0.02),
        'ev_w_rg_a': nrm((N_EVEN, LRU_HEADS, LRU_HEAD_DIM, LRU_HEAD_DIM), LRU_HEAD_DIM ** -0.5),
        'ev_b_rg_a': nrm((N_EVEN, LRU_WIDTH), 0.02),
        'ev_w_rg_x': nrm((N_EVEN, LRU_HEADS, LRU_HEAD_DIM, LRU_HEAD_DIM), LRU_HEAD_DIM ** -0.5),
        'ev_b_rg_x': nrm((N_EVEN, LRU_WIDTH), 0.02),
        'ev_lru_lambda': lru_lambda,
        'ev_q_norm_g': 1.0 + nrm((N_EVEN, MLA_Q_RANK), 0.02),
        'ev_w_uq': nrm((N_EVEN, MLA_Q_RANK, MLA_HEADS * (MLA_NOPE + MLA_ROPE)), MLA_Q_RANK ** -0.5),
        'ev_kv_norm_g': 1.0 + nrm((N_EVEN, MLA_KV_RANK), 0.02),
        'ev_w_ukv': nrm((N_EVEN, MLA_KV_RANK, MLA_HEADS * (MLA_NOPE + MLA_V)), MLA_KV_RANK ** -0.5),
        'ev_w_out': nrm((N_EVEN, EVEN_MIX, D_MODEL), DN_BETA * EVEN_MIX ** -0.5),
        'od_w_in': nrm((N_ODD, D_MODEL, RET_IN), D_MODEL ** -0.5),
        'od_w_out': nrm((N_ODD, RET_MIX, D_MODEL), DN_BETA * RET_MIX ** -0.5),
        'ln_mix_g': 1.0 + nrm((DEPTH, D_MODEL), 0.02),
        'ln_mix_b': nrm((DEPTH, D_MODEL), 0.02),
        'mlp_w1': nrm((DEPTH, D_MODEL, D_FF), D_MODEL ** -0.5),
        'mlp_w2': nrm((DEPTH, D_FF, D_MODEL), DN_BETA * D_FF ** -0.5),
        'ln_mlp_g': 1.0 + nrm((DEPTH, D_MODEL), 0.02),
        'ln_mlp_b': nrm((DEPTH, D_MODEL), 0.02),
    }


def reference(x, meta_tokens, ev_w_in, ev_conv_w, ev_conv_b, ev_w_rg_a, ev_b_rg_a, ev_w_rg_x,
              ev_b_rg_x, ev_lru_lambda, ev_q_norm_g, ev_w_uq, ev_kv_norm_g, ev_w_ukv, ev_w_out,
              od_w_in, od_w_out, ln_mix_g, ln_mix_b, mlp_w1, mlp_w2, ln_mlp_g, ln_mlp_b):
    B = x.shape[0]
    meta = jnp.broadcast_to(meta_tokens[None].astype(x.dtype), (B, N_META, D_MODEL))
    h = jnp.concatenate([meta, x], axis=1)
    pos = jnp.arange(h.shape[1], dtype=jnp.int32)
    for l in range(DEPTH):
        if l % 2 == 0:
            e = l // 2
            mix = _even_mixer(h, pos, ev_w_in[e], ev_conv_w[e], ev_conv_b[e], ev_w_rg_a[e], ev_b_rg_a[e],
                              ev_w_rg_x[e], ev_b_rg_x[e], ev_lru_lambda[e], ev_q_norm_g[e], ev_w_uq[e],
                              ev_kv_norm_g[e], ev_w_ukv[e], ev_w_out[e])
        else:
            o = l // 2
            mix = _odd_mixer(h, pos, od_w_in[o], od_w_out[o])
        h = _layernorm(DN_ALPHA * h + mix, ln_mix_g[l], ln_mix_b[l])
        f = jnp.square(jax.nn.relu(h @ mlp_w1[l])) @ mlp_w2[l]
        h = _layernorm(DN_ALPHA * h + f, ln_mlp_g[l], ln_mlp_b[l])
    return h[:, N_META:]
```

```python
import math
import numpy as np
import concourse.bass as bass
import concourse.mybir as mybir
from concourse.bass_utils import run_bass_kernel_spmd

F32 = mybir.dt.float32
BF16 = mybir.dt.bfloat16
AF = mybir.ActivationFunctionType
ALU = mybir.AluOpType

NCORES = 8
SEQ_PER_CORE = 2
D = 1024
T = 2064
NMETA = 16
ALPHA = 4.0 ** 0.25
EPS = 1e-5
TILES = [(0, 16)] + [(16 + 512 * i, 512) for i in range(4)]
BLOCKS = [(0, 16)] + [(16 + 128 * j, 128) for j in range(16)]
GEN = 3000
FUSE_WAIT = True


def tile_of_block(bi):
    return 0 if bi == 0 else 1 + (bi - 1) // 4


class Op:
    __slots__ = ("eng", "fn", "deps", "dma", "semkey", "signal", "cnt", "dcount", "clock", "needed", "rdeps")


class Prog:
    ENGS = ("pe", "act", "dve", "pool", "sp")

    def __init__(self, nc):
        self.nc = nc
        self.q = {e: [] for e in self.ENGS}
        self.lastw = {}
        self.readers = {}
        self.dma_counts = {}
        self.last_op = {e: None for e in self.ENGS}
        self.pending_barrier = {e: [] for e in self.ENGS}
        self.all_ops = []
        self.dma_ops = {}

    def op(self, eng, fn, r=(), w=(), dma=False, semkey=None, mk=()):
        o = Op()
        o.eng, o.fn, o.dma, o.signal = eng, fn, dma, False
        o.cnt = 0
        deps = {}

        def add(d, raw):
            if d.dma:
                deps[("d", d.semkey)] = ("d", d.semkey, self.dma_counts[d.semkey])
                return
            if d.eng == eng and not dma:
                if eng == "pe":
                    return
            deps[("c", id(d))] = ("c", d)
            d.signal = True

        r_all = r
        mk = set(mk)
        r = [k for k in r_all if k not in mk] + [None] + [k for k in r_all if k in mk]
        rdeps = None
        for k in r:
            if k is None:
                rdeps = set(deps.keys())
                continue
            lw = self.lastw.get(k)
            if lw is not None:
                add(lw, True)
            if isinstance(k, tuple) and k[0] == "ps":
                for rd in self.readers.get(k, {}).values():
                    if rd.eng != eng:
                        add(rd, False)
        r = r_all
        for k in w:
            lw = self.lastw.get(k)
            if lw is not None:
                add(lw, False)
            for rd in self.readers.get(k, {}).values():
                add(rd, False)
        for dep in self.pending_barrier[eng]:
            if dep[0] == "c":
                deps[("c", id(dep[1]))] = dep
            else:
                old_ = deps.get(("d", dep[1]))
                if old_ is None or old_[2] < dep[2]:
                    deps[("d", dep[1])] = dep
            rdeps.add(("c", id(dep[1])) if dep[0] == "c" else ("d", dep[1]))
        self.pending_barrier[eng] = []
        o.deps = list(deps.values())
        o.rdeps = rdeps
        if dma:
            o.semkey = semkey if semkey is not None else w[0]
            self.dma_counts[o.semkey] = self.dma_counts.get(o.semkey, 0) + 1
            o.dcount = self.dma_counts[o.semkey]
            self.dma_ops[(o.semkey, o.dcount)] = o
        for k in r:
            self.readers.setdefault(k, {})[("d", id(o)) if dma else eng] = o
        for k in w:
            self.lastw[k] = o
            self.readers[k] = {}
        self.q[eng].append(o)
        self.all_ops.append(o)
        if not dma:
            self.last_op[eng] = o
        return o

    def barrier(self, keep_prefixes=()):
        deps = []
        for e in self.ENGS:
            lo = self.last_op[e]
            if lo is not None:
                lo.signal = True
                deps.append(("c", lo))
        for k, c in self.dma_counts.items():
            deps.append(("d", k, c))
        for e in self.ENGS:
            self.pending_barrier[e] = [d for d in deps if not (d[0] == "c" and d[1].eng == e and e == "pe")] + \
                self.pending_barrier[e]
        self.lastw = {}
        self.readers = {}

    def emit(self):
        nc = self.nc
        csem = {}
        dsem = {}

        def get_csem(e, g):
            if (e, g) not in csem:
                csem[(e, g)] = nc.alloc_semaphore(f"c_{e}_{g}")
            return csem[(e, g)]

        def get_dsem(k):
            if k not in dsem:
                dsem[k] = nc.alloc_semaphore("d_" + "_".join(str(x) for x in (k if isinstance(k, tuple) else (k,))))
            return dsem[k]

        for e in self.ENGS:
            c = 0
            for o in self.q[e]:
                if o.signal and not o.dma:
                    c += 1
                    o.cnt = c

        known = {e: {} for e in self.ENGS}

        def merge(dst, src):
            for k_, v_ in src.items():
                if dst.get(k_, 0) < v_:
                    dst[k_] = v_

        for o in self.all_ops:
            kn = known[o.eng]
            needed = []
            for dep in o.deps:
                if dep[0] == "c":
                    key, val, src = ("c", dep[1].eng), dep[1].cnt, dep[1]
                else:
                    key, val, src = ("d", dep[1]), dep[2], self.dma_ops.get((dep[1], dep[2]))
                if kn.get(key, 0) >= val:
                    continue
                needed.append(dep)
                kn[key] = val
                if src is not None and getattr(src, "clock", None) is not None:
                    merge(kn, src.clock)
            o.needed = needed
            o.clock = dict(kn)

        def run(e):
            def body(engine):
                waited = {}
                for o in self.q[e]:
                    need = []
                    pe_fusable = []
                    for dep in o.needed:
                        if dep[0] == "c":
                            d = dep[1]
                            g = (d.cnt - 1) // GEN
                            sem = get_csem(d.eng, g)
                            key = ("c", d.eng, g)
                            val = d.cnt - g * GEN
                        else:
                            sem = get_dsem(dep[1])
                            key = ("d", dep[1])
                            val = 16 * dep[2]
                        if waited.get(key, 0) < val:
                            dk = ("c", id(dep[1])) if dep[0] == "c" else ("d", dep[1])
                            if e == "pe" and dk not in o.rdeps:
                                pe_fusable.append((sem, val))
                            else:
                                need.append((sem, val))
                            waited[key] = val
                    if e == "pe" and pe_fusable:
                        need = need + pe_fusable[:-1]
                        for sem, val in need:
                            engine.wait_ge(sem, val)
                        ins = o.fn(engine)
                        ins._wait_ge(pe_fusable[-1][0], pe_fusable[-1][1])
                        if o.signal:
                            g = (o.cnt - 1) // GEN
                            ins.then_inc(get_csem(e, g), 1)
                        continue
                    fuse = FUSE_WAIT and (not o.dma) and e in ("act", "dve", "pool") and len(need) > 0
                    for sem, val in (need[:-1] if fuse else need):
                        engine.wait_ge(sem, val)
                    ins = o.fn(engine)
                    if fuse:
                        ins._wait_ge(need[-1][0], need[-1][1])
                    if o.dma:
                        ins.then_inc(get_dsem(o.semkey), 16)
                    elif o.signal:
                        g = (o.cnt - 1) // GEN
                        ins.then_inc(get_csem(e, g), 1)
                if e == "sp":
                    for k, cnt in self.dma_counts.items():
                        engine.wait_ge(get_dsem(k), 16 * cnt)
            return body

        with nc.Block() as block:
            block.sync(run("sp"))
            block.gpsimd(run("pool"))
            block.scalar(run("act"))
            block.vector(run("dve"))
            block.tensor(run("pe"))


class Arena:
    def __init__(self, nc, base, top):
        self.nc, self.base, self.top, self.cur = nc, base, top, base
        self.n = 0

    def reset(self):
        self.cur = self.base

    def alloc(self, name, shape, dtype):
        esz = 4 if dtype == F32 else 2
        per = esz
        for s in shape[1:]:
            per *= s
        off = (self.cur + 63) // 64 * 64
        assert off + per <= self.top, f"arena overflow {name}: need {per} at {off}, top {self.top}"
        self.cur = off + per
        self.n += 1
        return self.nc.alloc_sbuf_tensor_at(f"{name}_{self.n}", list(shape), dtype, offset=off)


def build(stage="full"):
    import os
    KVAR = os.environ.get('KVAR', '')
    nc = bass.Bass("TRN2", target_bir_lowering=False)
    P = Prog(nc)

    USE = {
        "ev": ("full", "l0", "lru", "mla"), "od": ("full", "ret"), "mlp": ("full", "l0", "mlp"),
    }
    in_names = []

    def din(name, shape):
        grp = name.split("_")[0]
        if grp in USE and stage not in USE[grp]:
            return None
        in_names.append(name)
        return nc.dram_tensor(name, list(shape), F32, kind="ExternalInput").ap()

    x_d = din("x", [SEQ_PER_CORE, 2048, D])
    meta_d = din("meta_tokens", [16, D])
    ev_w_in = din("ev_w_in", [D, 1440])
    ev_w_rg_a = din("ev_w_rg_a", [4, 128, 128])
    ev_w_rg_x = din("ev_w_rg_x", [4, 128, 128])
    ev_w_uq = din("ev_w_uq", [256, 768])
    ev_w_ukv = din("ev_w_ukv", [128, 1024])
    ev_w_out = din("ev_w_out", [1024, 1024])
    od_w_in = din("od_w_in", [D, 6144])
    od_w_out = din("od_w_out", [2048, D])
    smallp_d = din("smallp", [128, 99])
    mlp_w1 = din("mlp_w1", [2, D, 4096])
    mlp_w2 = din("mlp_w2", [2, 4096, D])
    c_ident = din("c_ident", [128, 128])
    c_mask = din("c_mask", [128, 128])
    c_rope0 = din("c_rope0", [2, 32, T])
    c_rope1 = din("c_rope1", [2, 128, T])
    c_dm = din("c_dm", [4, 128, 512])
    c_qdec = din("c_qdec", [4, 128, 512])
    c_kdec = din("c_kdec", [128, 20])
    out_d = nc.dram_tensor("out", [SEQ_PER_CORE, 2048, D], F32, kind="ExternalOutput").ap()

    GAM = [1.0 - 2.0 ** (-5.0 - h) for h in range(4)]

    SB_BASE, SB_TOP = 16512, 229344
    fixed = Arena(nc, SB_BASE, SB_TOP)
    R = fixed.alloc("R", [128, 8, T], F32)
    hb = fixed.alloc("hb", [128, 8, T], BF16)
    ident = fixed.alloc("ident", [128, 128], F32)
    identb = fixed.alloc("identb", [128, 128], BF16)
    onesb = fixed.alloc("onesb", [128, 128], BF16)
    maskb = fixed.alloc("maskb", [128, 128], BF16)
    maskf = fixed.alloc("maskf", [128, 128], F32)
    small = fixed.alloc("small", [128, 99], F32)
    lna = fixed.alloc("lna", [128, 64], F32)
    epst = fixed.alloc("epst", [128, 1], F32)
    LN_BYTES = 23040
    LNA = Arena(nc, SB_TOP - LN_BYTES, SB_TOP)
    ln_rb = LNA.alloc("rb", [128, 8, 512], BF16)
    ln_rsq = LNA.alloc("rsq", [128, 8, 512], BF16)
    ln_mean = LNA.alloc("mean", [128, 512], F32)
    ln_var = LNA.alloc("var", [128, 512], F32)
    ln_rstd = LNA.alloc("rstd", [128, 512], F32)
    A = Arena(nc, fixed.cur, SB_TOP)

    ps = [nc.alloc_psum_tensor(f"ps{i}", [128, 512], F32) for i in range(8)]

    def PK(i):
        return ("ps", i)

    def mm(out, lhsT, rhs, start, stop, r, w, mk=()):
        return P.op("pe", lambda e: e.matmul(out, lhsT, rhs, start=start, stop=stop), r=r, w=w, mk=mk)

    def tr(out, in_, idn, r, w):
        return P.op("pe", lambda e: e.transpose(out, in_, idn), r=r, w=w)

    def act(out, in_, func, r, w, bias=None, scale=1.0):
        if bias is None:
            return P.op("act", lambda e: e.activation(out=out, in_=in_, func=func, scale=scale), r=r, w=w)
        return P.op("act", lambda e: e.activation(out=out, in_=in_, func=func, bias=bias, scale=scale), r=r, w=w)

    def tt(eng, out, in0, in1, op, r, w):
        return P.op(eng, lambda e: e.tensor_tensor(out=out, in0=in0, in1=in1, op=op), r=r, w=w)

    def ts(eng, out, in0, s1, s2, op0, op1, r, w):
        return P.op(eng, lambda e: e.tensor_scalar(out=out, in0=in0, scalar1=s1, scalar2=s2, op0=op0, op1=op1),
                    r=r, w=w)

    def stt(out, in0, scalar, in1, op0, op1, r, w):
        return P.op("dve", lambda e: e.scalar_tensor_tensor(out=out, in0=in0, scalar=scalar, in1=in1,
                                                            op0=op0, op1=op1), r=r, w=w)

    def dma(eng, out, in_, r, w, semkey=None):
        return P.op(eng, lambda e: e.dma_start(out=out, in_=in_), r=r, w=w, dma=True, semkey=semkey)

    def recip(out, in_, r, w):
        return P.op("dve", lambda e: e.reciprocal(out=out, in_=in_), r=r, w=w)

    def Rk(c, ti):
        return ("R", c, ti)

    def Hk(c, ti):
        return ("hb", c, ti)

    dma("sp", ident[:, :], c_ident, r=[], w=["ident"])
    dma("sp", maskf[:, :], c_mask, r=[], w=["maskf"])
    if 'noconst' not in KVAR:
        P.op("dve", lambda e: e.tensor_copy(out=identb[:, :], in_=ident[:, :]), r=["ident"], w=["identb"])
        P.op("dve", lambda e: e.tensor_copy(out=maskb[:, :], in_=maskf[:, :]), r=["maskf"], w=["maskb"])
        P.op("dve", lambda e: e.memset(onesb[:, :], 1.0), r=[], w=["onesb"])
        P.op("dve", lambda e: e.memset(epst[:, :], EPS), r=[], w=["epst"])
        dma("sp", small[:, :], smallp_d, r=[], w=["small"])
        act(lna[:, :], small[:, 35:99], AF.Copy, r=["small"], w=["lna"], scale=ALPHA)
    P.barrier()

    XA = Arena(nc, SB_TOP - 4160, SB_TOP)

    def load_gen(s):
        XA.reset()
        xin = [XA.alloc("xin", [128, D], F32)] * 2
        for bi, (t0, n) in enumerate(BLOCKS):
            if 'nometa' in KVAR and bi == 0:
                continue
            ti = tile_of_block(bi)
            src = meta_d[0:16, :] if bi == 0 else x_d[s, t0 - 16:t0 - 16 + n, :]
            xt = xin[bi % 2]
            dma("sp", xt[:n, :], src, r=[], w=[("xin", 0)])
            for half in range(2):
                pb = (bi * 2 + half) % 4
                for c4 in range(4):
                    c = half * 4 + c4
                    tr(ps[pb][:, c4 * 128:c4 * 128 + n], xt[:n, c * 128:(c + 1) * 128], ident[:n, :n],
                       r=[("xin", 0), "ident"], w=[PK(pb)])
                pv = ps[pb][:, :].rearrange("p (c t) -> p c t", c=4)[:, :, 0:n]
                act(R[:, half * 4:half * 4 + 4, t0:t0 + n], pv, AF.Copy, r=[PK(pb)],
                    w=[Rk(c, ti) for c in range(half * 4, half * 4 + 4)], scale=ALPHA)
                P.op("dve", lambda e, o=hb[:, half * 4:half * 4 + 4, t0:t0 + n], i=pv: e.tensor_copy(out=o, in_=i),
                     r=[PK(pb)], w=[Hk(c, ti) for c in range(half * 4, half * 4 + 4)])
            yield

    def phase_load(s):
        g = load_gen(s)
        while step(g):
            pass
        P.barrier()

    def phase_store(s):
        A.reset()
        xo = [A.alloc("xo", [128, D], F32) for _ in range(2)]
        for bi, (t0, n) in enumerate(BLOCKS):
            if bi == 0:
                continue
            ti = tile_of_block(bi)
            xt = xo[bi % 2]
            for half in range(2):
                pb = (bi * 2 + half) % 4
                for c4 in range(4):
                    c = half * 4 + c4
                    tr(ps[pb][:n, c4 * 128:(c4 + 1) * 128], R[:, c, t0:t0 + n], ident[:, :],
                       r=[Rk(c, ti), "ident"], w=[PK(pb)])
                if half == 0:
                    act(xt[:n, 0:512], ps[pb][:n, :], AF.Copy, r=[PK(pb)], w=[("xo", bi % 2, 0)])
                else:
                    P.op("dve", lambda e, o=xt[:n, 512:1024], i=ps[pb][:n, :]: e.tensor_copy(out=o, in_=i),
                         r=[PK(pb)], w=[("xo", bi % 2, 1)])
            dma("sp", out_d[s, t0 - 16:t0 - 16 + n, :], xt[:n, :], r=[("xo", bi % 2, 0), ("xo", bi % 2, 1)],
                w=[("outd", bi)], semkey=("st", bi % 2))
        P.barrier()

    def ln_gen(li, final=False):
        rb = [ln_rb] * 2
        rsq = [ln_rsq] * 2
        mean = [ln_mean] * 2
        var = [ln_var] * 2
        rstd = [ln_rstd] * 2
        def gcol(kind, c, scaled):
            t_ = lna if scaled else small
            o_ = (0 if scaled else 35) + kind * 32 + li * 8 + c
            return t_[:, o_:o_ + 1]
        for ti, (t0, n) in enumerate(TILES):
            b = 0
            for c in range(8):
                act(rb[b][:, c, :n], R[:, c, t0:t0 + n], AF.Copy, r=[Rk(c, ti)], w=[("rb", b, c)])
                act(rsq[b][:, c, :n], R[:, c, t0:t0 + n], AF.Square, r=[Rk(c, ti)], w=[("rsq", b, c)])
            p1, p2 = 6 + 0 * ti, 7
            for c in range(8):
                mm(ps[p1][:, :n], onesb[:, :], rb[b][:, c, :n], c == 0, c == 7, r=[("rb", b, c), "onesb"], w=[PK(p1)])
            for c in range(8):
                mm(ps[p2][:, :n], onesb[:, :], rsq[b][:, c, :n], c == 0, c == 7, r=[("rsq", b, c), "onesb"],
                   w=[PK(p2)])
            act(mean[b][:, :n], ps[p1][:, :n], AF.Copy, r=[PK(p1)], w=[("mean", b)], scale=1.0 / D)
            tt("dve", var[b][:, :n], mean[b][:, :n], mean[b][:, :n], ALU.mult, r=[("mean", b)], w=[("var", b)])
            stt(var[b][:, :n], ps[p2][:, :n], 1.0 / D, var[b][:, :n], ALU.mult, ALU.subtract,
                r=[PK(p2), ("var", b)], w=[("var", b)])
            act(rstd[b][:, :n], var[b][:, :n], AF.Sqrt, r=[("var", b), "epst"], w=[("rstd", b)], bias=epst[:, :])
            recip(rstd[b][:, :n], rstd[b][:, :n], r=[("rstd", b)], w=[("rstd", b)])
            for c in range(8):
                tt("dve", R[:, c, t0:t0 + n], R[:, c, t0:t0 + n], mean[b][:, :n], ALU.subtract,
                   r=[Rk(c, ti), ("mean", b)], w=[Rk(c, ti)])
            for c in range(8):
                tt("pool", R[:, c, t0:t0 + n], R[:, c, t0:t0 + n], rstd[b][:, :n], ALU.mult,
                   r=[Rk(c, ti), ("rstd", b)], w=[Rk(c, ti)])
            for c in range(8):
                if not final:
                    ts("dve", hb[:, c, t0:t0 + n], R[:, c, t0:t0 + n], gcol(0, c, False), gcol(1, c, False),
                       ALU.mult, ALU.add, r=[Rk(c, ti)], w=[Hk(c, ti)])
                act(R[:, c, t0:t0 + n], R[:, c, t0:t0 + n], AF.Identity, r=[Rk(c, ti)], w=[Rk(c, ti)],
                    bias=gcol(1, c, not final), scale=gcol(0, c, not final))
            yield

    def step(g):
        try:
            next(g)
            return True
        except StopIteration:
            return False

    def phase_ln(li, final=False, end_barrier=True):
        g = ln_gen(li, final)
        while step(g):
            pass
        if end_barrier:
            P.barrier()

    mlp_state = {}

    def mlp_prefetch(l):
        A.reset()
        A.top = SB_TOP - LN_BYTES
        mlp_state["pre"] = True
        phase_mlp(l, prefetch_only=True)

    def phase_mlp(l, end_barrier=True, prefetch_only=False):
        if not prefetch_only and not mlp_state.get("pre"):
            A.reset()
            A.top = SB_TOP - LN_BYTES
        if not prefetch_only and mlp_state.get("pre"):
            mlp_state["pre"] = False
            W1, W2, ub, tmpb, load = mlp_state["bufs"]
            g_ = mlp_main(l, end_barrier, W1, W2, ub, tmpb, load)
            while step(g_):
                pass
            return
        NG = 4
        W1 = [A.alloc("W1", [128, 8, 1024], BF16) for _ in range(2)]
        W2 = [A.alloc("W2", [128, 8, 1024], BF16) for _ in range(2)]
        ub = [A.alloc("ub", [128, 8, 512], BF16) for _ in range(2)]
        tmpb = [A.alloc("tmpb", [128, 512], BF16) for _ in range(3)]
        w1v = mlp_w1[l].rearrange("(kc p) n -> p kc n", p=128)
        w2v = mlp_w2[l].rearrange("(fc p) n -> p fc n", p=128)

        def load(g):
            sl = g % 2
            for hlf in range(2):
                dma("pool", W1[sl][:, 4 * hlf:4 * hlf + 4, :], w1v[:, 4 * hlf:4 * hlf + 4, g * 1024:(g + 1) * 1024],
                    r=[], w=[("W1", sl)])
            for hlf in range(2):
                dma("pool", W2[sl][:, 4 * hlf:4 * hlf + 4, :], w2v[:, g * 8 + 4 * hlf:g * 8 + 4 * hlf + 4, :],
                    r=[], w=[("W2", sl)])

        load(0)
        if prefetch_only:
            mlp_state["bufs"] = (W1, W2, ub, tmpb, load)
            return
        g_ = mlp_main(l, end_barrier, W1, W2, ub, tmpb, load, loaded0=True)
        while step(g_):
            pass

    def mlp_main(l, end_barrier, W1, W2, ub, tmpb, load, loaded0=True):
        NG = 4
        ucnt = 0
        fcnt = 0
        it = 0
        for g in range(NG):
            sl = g % 2
            if g + 1 < NG:
                load(g + 1)
            for ti, (t0, n) in enumerate(TILES):
                us = it % 2
                it += 1
                for j in range(8):
                    pb = ucnt % 2
                    tb = ucnt % 3
                    ucnt += 1
                    for kc in range(8):
                        mm(ps[pb][:, :n], W1[sl][:, kc, j * 128:(j + 1) * 128], hb[:, kc, t0:t0 + n], kc == 0, kc == 7,
                           r=[("W1", sl), Hk(kc, ti)], w=[PK(pb)], mk=[Hk(kc, ti)])
                    act(tmpb[tb][:, :n], ps[pb][:, :n], AF.Relu, r=[PK(pb)], w=[("tmpb", tb)])
                    tt("dve", ub[us][:, j, :n], tmpb[tb][:, :n], tmpb[tb][:, :n], ALU.mult, r=[("tmpb", tb)],
                       w=[("ub", us, j)])
                for dc in range(8):
                    pb = 2 + fcnt % 4
                    fcnt += 1
                    for j in range(8):
                        mm(ps[pb][:, :n], W2[sl][:, j, dc * 128:(dc + 1) * 128], ub[us][:, j, :n], j == 0, j == 7,
                           r=[("W2", sl), ("ub", us, j)], w=[PK(pb)], mk=[("ub", us, j)])
                    tt("dve", R[:, dc, t0:t0 + n], R[:, dc, t0:t0 + n], ps[pb][:, :n], ALU.add,
                       r=[Rk(dc, ti), PK(pb)], w=[Rk(dc, ti)])
                yield
        A.top = SB_TOP
        if end_barrier:
            P.barrier()

    def seq_ln_mlp_ln(l, li_a, li_b, final_b):
        mlp_prefetch(l)
        mlp_state["pre"] = False
        W1, W2, ub, tmpb, load = mlp_state["bufs"]
        ga = ln_gen(li_a, False)
        gm = mlp_main(l, False, W1, W2, ub, tmpb, load)
        gb = ln_gen(li_b, final_b)
        step(ga)
        step(ga)
        for t in range(5):
            step(gm)
            if t + 2 < 5:
                step(ga)
        for _ in range(10):
            step(gm)
        step(gm)
        step(gm)
        for t in range(5):
            step(gb)
            if t + 2 < 5:
                step(gm)
        while step(gm):
            pass
        while step(ga):
            pass
        while step(gb):
            pass
        P.barrier()

    def phase_ret():
        A.reset()
        Whqk = A.alloc("Whqk", [128, 8, 512], BF16)
        Whv = A.alloc("Whv", [128, 8, 512], BF16)
        Whg = A.alloc("Whg", [128, 8, 512], BF16)
        Woh = A.alloc("Woh", [128, 4, 1024], BF16)
        cosT = A.alloc("cosT", [128, T], F32)
        sinT = A.alloc("sinT", [128, T], F32)
        dmT = A.alloc("dmT", [128, 512], F32)
        qdec = A.alloc("qdec", [128, 512], F32)
        kdec = A.alloc("kdec", [128, 20], F32)
        qT = [A.alloc("qT", [128, 2, 512], BF16) for _ in range(2)]
        kT = [A.alloc("kT", [128, 2, 512], BF16) for _ in range(2)]
        qdT = A.alloc("qdT", [128, 2, 512], BF16)
        gsT = [A.alloc("gsT", [128, 4, 512], BF16) for _ in range(2)]
        vtok = [A.alloc("vtok", [128, 4, 512], BF16) for _ in range(2)]
        ktok = [A.alloc("ktok", [128, 4, 256], BF16) for _ in range(2)]
        yT = A.alloc("yT", [128, 4, 512], BF16)
        S32 = A.alloc("S32", [128, 2, 512], F32)
        Sb = [A.alloc("Sb", [128, 2, 512], BF16) for _ in range(2)]
        PT = A.alloc("PT", [128, 1280], BF16)
        osq = [A.alloc("osq", [128, 512], BF16) for _ in range(2)]
        of = A.alloc("of", [128, 512], F32)
        rs = A.alloc("rs", [128, 512], F32)
        t1 = A.alloc("t1", [128, 512], F32)
        t2 = A.alloc("t2", [128, 512], F32)
        wv = od_w_in.rearrange("(kc p) n -> p kc n", p=128)
        wov = od_w_out.rearrange("(fc p) n -> p fc n", p=128)

        dma("sp", cosT[:, :], c_rope1[0], r=[], w=["cosT"])
        dma("sp", sinT[:, :], c_rope1[1], r=[], w=["sinT"])
        dma("sp", kdec[:, :], c_kdec, r=[], w=["kdec"])

        def load_qk(h):
            dma("pool", Whqk[:, :, 0:256], wv[:, :, h * 256:(h + 1) * 256], r=[], w=["Whqk"])
            dma("pool", Whqk[:, :, 256:512], wv[:, :, 1024 + h * 256:1024 + (h + 1) * 256], r=[], w=["Whqk"])

        def load_g(h):
            dma("pool", Whg[:, :, :], wv[:, :, 4096 + h * 512:4096 + (h + 1) * 512], r=[], w=["Whg"])

        def load_v(h):
            dma("pool", Whv[:, :, :], wv[:, :, 2048 + h * 512:2048 + (h + 1) * 512], r=[], w=["Whv"])

        def load_o(h):
            dma("pool", Woh[:, :, :], wov[:, h * 4:(h + 1) * 4, :], r=[], w=["Woh"])

        def load_tabs(h):
            dma("sp", dmT[:, :], c_dm[h], r=[], w=["dmT"])
            dma("sp", qdec[:, :], c_qdec[h], r=[], w=["qdec"])

        load_qk(0)
        load_g(0)
        load_v(0)
        load_o(0)
        cnt = {"p": 0}

        def blocks_of(ti):
            if ti == 0:
                return [(0, 16)]
            return [(128 * j, 128) for j in range(4)]

        def front(h, ti, par):
            t0, n = TILES[ti]
            last = (ti == 4)
            for which, (dst, coff) in enumerate(((qT[par], 0), (kT[par], 256))):
                xb = []
                for dcq in range(2):
                    pb = dcq
                    xb.append(pb)
                    for kc in range(8):
                        mm(ps[pb][:, :n], Whqk[:, kc, coff + dcq * 128:coff + (dcq + 1) * 128],
                           hb[:, kc, t0:t0 + n], kc == 0, kc == 7, r=["Whqk", Hk(kc, ti)], w=[PK(pb)], mk=[Hk(kc, ti)])
                if last and which == 1 and h + 1 < 4:
                    load_qk(h + 1)
                cs_, sn_ = cosT[:, t0:t0 + n], sinT[:, t0:t0 + n]
                x1, x2 = ps[xb[0]][:, :n], ps[xb[1]][:, :n]
                tt("dve", t1[:, :n], x1, cs_, ALU.mult, r=[PK(xb[0]), "cosT"], w=["t1"])
                tt("dve", t2[:, :n], x2, sn_, ALU.mult, r=[PK(xb[1]), "sinT"], w=["t2"])
                tt("dve", dst[:, 0, :n], t1[:, :n], t2[:, :n], ALU.subtract, r=["t1", "t2"],
                   w=[("qk", par, which, 0)])
                tt("dve", t1[:, :n], x1, sn_, ALU.mult, r=[PK(xb[0]), "sinT"], w=["t1"])
                tt("dve", t2[:, :n], x2, cs_, ALU.mult, r=[PK(xb[1]), "cosT"], w=["t2"])
                tt("dve", dst[:, 1, :n], t1[:, :n], t2[:, :n], ALU.add, r=["t1", "t2"],
                   w=[("qk", par, which, 1)])
                yield
            for fc in range(4):
                pb = fc % 2
                for kc in range(8):
                    mm(ps[pb][:, :n], Whg[:, kc, fc * 128:(fc + 1) * 128], hb[:, kc, t0:t0 + n],
                       kc == 0, kc == 7, r=["Whg", Hk(kc, ti)], w=[PK(pb)], mk=[Hk(kc, ti)])
                act(gsT[par][:, fc, :n], ps[pb][:, :n], AF.Silu, r=[PK(pb)], w=[("gsT", par, fc)])
                if fc % 2 == 1:
                    yield
            if last and h + 1 < 4:
                load_g(h + 1)
            for b, (cs, cn) in enumerate(blocks_of(ti)):
                pb = b % 2
                for kc in range(8):
                    mm(ps[pb][:cn, :], hb[:, kc, t0 + cs:t0 + cs + cn], Whv[:, kc, :], kc == 0, kc == 7,
                       r=["Whv", Hk(kc, ti)], w=[PK(pb)])
                act(vtok[par][:cn, b, :], ps[pb][:cn, :], AF.Copy, r=[PK(pb)], w=[("vtok", par, b)])
                if b % 2 == 1:
                    yield
            if last and h + 1 < 4:
                load_v(h + 1)
            psb = ps[7][:, :].bitcast(BF16)
            for b, (cs, cn) in enumerate(blocks_of(ti)):
                for dcq in range(2):
                    tr(psb[:cn, b * 256 + dcq * 128:b * 256 + (dcq + 1) * 128], kT[par][:, dcq, cs:cs + cn],
                       identb[:, :], r=[("qk", par, 1, dcq), "identb"], w=[PK(7)])
            for b, (cs, cn) in enumerate(blocks_of(ti)):
                kcol = h * 5 + (4 if ti == 0 else b)
                act(ktok[par][:cn, b, :], psb[:cn, b * 256:(b + 1) * 256], AF.Identity, r=[PK(7), "kdec"],
                    w=[("ktok", par, b)], scale=kdec[:cn, kcol:kcol + 1])
            yield

        def back(h, ti, par, spar, first):
            t0, n = TILES[ti]
            blks = blocks_of(ti)
            nb = len(blks)
            for dcq in range(2):
                pbs = 5 + dcq
                for b, (cs, cn) in enumerate(blks):
                    mm(ps[pbs][:, :], ktok[par][:cn, b, dcq * 128:(dcq + 1) * 128], vtok[par][:cn, b, :],
                       b == 0, b == nb - 1, r=[("ktok", par, b), ("vtok", par, b)], w=[PK(pbs)])
                if first:
                    act(S32[:, dcq, :], ps[pbs][:, :], AF.Copy, r=[PK(pbs)], w=[("S32", dcq)])
                else:
                    stt(S32[:, dcq, :], S32[:, dcq, :], GAM[h] ** 512, ps[pbs][:, :], ALU.mult, ALU.add,
                        r=[("S32", dcq), PK(pbs)], w=[("S32", dcq)])
            act(Sb[1 - spar][:, :, :], S32[:, :, :], AF.Copy, r=[("S32", 0), ("S32", 1)], w=[("Sb", 1 - spar)])
            for dcq in range(2):
                tt("dve", qdT[:, dcq, :n], qT[par][:, dcq, :n], qdec[:, :n], ALU.mult,
                   r=[("qk", par, 0, dcq), "qdec"], w=[("qdT", dcq)])
            yield
            offs = []
            off = 0
            for jb, (cs, cn) in enumerate(blks):
                nn = n - cs
                sb_ = 2 + jb % 2
                for dcq in range(2):
                    mm(ps[sb_][:cn, :nn], kT[par][:, dcq, cs:cs + cn], qT[par][:, dcq, cs:n], dcq == 0, dcq == 1,
                       r=[("qk", par, 1, dcq), ("qk", par, 0, dcq)], w=[PK(sb_)])
                tt("dve", PT[:cn, off:off + nn], ps[sb_][:cn, :nn], dmT[:cn, :nn], ALU.mult, r=[PK(sb_), "dmT"],
                   w=[("PT", jb)])
                offs.append((off, nn))
                off += nn
                if jb % 2 == 1:
                    yield
            yield
            for fc in range(4):
                ob = (4, 2)[fc % 2]
                started = False
                if not first:
                    for dcq in range(2):
                        mm(ps[ob][:, :n], Sb[spar][:, dcq, fc * 128:(fc + 1) * 128], qdT[:, dcq, :n],
                           dcq == 0, False, r=[("Sb", spar), ("qdT", dcq)], w=[PK(ob)])
                    started = True
                for jb, (cs, cn) in enumerate(blks):
                    o_, nn = offs[jb]
                    mm(ps[ob][:, cs:n], vtok[par][:cn, jb, fc * 128:(fc + 1) * 128], PT[:cn, o_:o_ + nn],
                       (not started) and jb == 0, jb == nb - 1, r=[("vtok", par, jb), ("PT", jb)], w=[PK(ob)])
                act(yT[:, fc, :n], ps[ob][:, :n], AF.Copy, r=[PK(ob)], w=[("yT", fc)])
                act(osq[fc % 2][:, :n], ps[ob][:, :n], AF.Square, r=[PK(ob)], w=[("osq", fc % 2)])
                mm(ps[3][:, :n], onesb[:, :], osq[fc % 2][:, :n], fc == 0, fc == 3, r=[("osq", fc % 2), "onesb"],
                   w=[PK(3)])
                yield
            act(rs[:, :n], ps[3][:, :n], AF.Sqrt, r=[PK(3), "epst"], w=["rs"], bias=epst[:, :], scale=1.0 / 512)
            recip(rs[:, :n], rs[:, :n], r=["rs"], w=["rs"])
            for fc in range(4):
                tt("dve", of[:, :n], yT[:, fc, :n], rs[:, :n], ALU.mult, r=[("yT", fc), "rs"], w=["of"])
                tt("dve", yT[:, fc, :n], of[:, :n], gsT[par][:, fc, :n], ALU.mult, r=["of", ("gsT", par, fc)],
                   w=[("yT", fc)])
            yield
            for dc in range(8):
                pb = 5 + dc % 2
                for fc in range(4):
                    mm(ps[pb][:, :n], Woh[:, fc, dc * 128:(dc + 1) * 128], yT[:, fc, :n], fc == 0, fc == 3,
                       r=["Woh", ("yT", fc)], w=[PK(pb)], mk=[("yT", fc)])
                tt("dve", R[:, dc, t0:t0 + n], R[:, dc, t0:t0 + n], ps[pb][:, :n], ALU.add,
                   r=[Rk(dc, ti), PK(pb)], w=[Rk(dc, ti)])
                if dc % 2 == 1:
                    yield
            if ti == 4 and h + 1 < 4:
                load_o(h + 1)

        def interleave(g1, g2):
            gens = [g for g in (g1, g2) if g is not None]
            while gens:
                for g in list(gens):
                    try:
                        next(g)
                    except StopIteration:
                        gens.remove(g)

        items = [(h, ti) for h in range(4) for ti in range(5)]
        interleave(front(0, 0, 0), None)
        for k, (h, ti) in enumerate(items):
            par = k % 2
            if ti == 0:
                load_tabs(h)
            nxt = front(items[k + 1][0], items[k + 1][1], 1 - par) if k + 1 < len(items) else None
            spar = ti % 2
            interleave(back(h, ti, par, spar, ti == 0), nxt)
        P.barrier()

    def phase_lru(loadg=None):
        A.reset()
        A.top = SB_TOP - 4160
        Wg = [A.alloc("Wg", [128, 8, 128], BF16) for _ in range(4)]
        Wr = [A.alloc("Wr", [128, 8, 128], BF16) for _ in range(4)]
        Wa = A.alloc("Wa", [128, 4, 128], BF16)
        Wx = A.alloc("Wx", [128, 4, 128], BF16)
        Wo = A.alloc("Wo", [128, 4, 1024], BF16)
        cneg = A.alloc("cneg", [128, 4], F32)
        yrT = A.alloc("yrT", [128, 4, T], BF16)
        prec_ = [A.alloc("prec", [128, T + 3], F32) for _ in range(2)]
        names = ("xc", "rg", "ig", "aa", "mu", "bb", "xg", "ug", "hs0", "hs1")
        tmp = [{nm: A.alloc(nm, [128, 512], F32) for nm in names} for _ in range(2)]
        xcb_ = [A.alloc("xcb", [128, 512], BF16) for _ in range(2)]
        wiv = ev_w_in.rearrange("(kc p) n -> p kc n", p=128)
        for c in range(4):
            dma("pool", Wr[c][:, :, :], wiv[:, :, 512 + c * 128:512 + (c + 1) * 128], r=[], w=[("Wr", c)])
            dma("pool", Wg[c][:, :, :], wiv[:, :, c * 128:(c + 1) * 128], r=[], w=[("Wg", c)])
            if c == 1:
                dma("pool", Wa[:, :, :], ev_w_rg_a.rearrange("h i j -> i h j"), r=[], w=["Wa"])
                dma("pool", Wx[:, :, :], ev_w_rg_x.rearrange("h i j -> i h j"), r=[], w=["Wx"])
        dma("pool", Wo[:, :, :], ev_w_out[0:512, :].rearrange("(c p) n -> p c n", p=128), r=[], w=["Wo"])
        act(cneg[:, :], small[:, 28:32], AF.Exp, r=["small"], w=["cneg"], scale=-1.0)
        act(cneg[:, :], cneg[:, :], AF.Ln, r=["cneg"], w=["cneg"], bias=1.0)
        act(cneg[:, :], cneg[:, :], AF.Copy, r=["cneg"], w=["cneg"], scale=-8.0)
        hprm = A.alloc("hprm", [128, 12], F32)
        act(hprm[:, 0:8], small[:, 20:28], AF.Copy, r=["small"], w=["hprm"], scale=0.5)
        act(hprm[:, 8:12], cneg[:, :], AF.Copy, r=["cneg", "hprm"], w=["hprm"], scale=0.5)
        for q_ in range(2):
            P.op("dve", lambda e, q_=q_: e.memset(prec_[q_][:, 0:3], 0.0), r=[], w=[("prec_h", q_)])

        def stream(q_, chunks):
            tm = tmp[q_]
            xc, rg, ig, aa, mu, bb, xg, ug = (tm[k_] for k_ in ("xc", "rg", "ig", "aa", "mu", "bb", "xg", "ug"))
            xcb = xcb_[q_]
            prec = prec_[q_]
            K_ = lambda nm: (nm, q_)
            pbase = 4 * q_
            pcnt = 0
            hcnt = 0
            for c in chunks:
                prev = None
                for ti, (t0, n) in enumerate(TILES):
                    pb = pbase + pcnt % 2
                    pcnt += 1
                    for kc in range(8):
                        mm(ps[pb][:, :n], Wr[c][:, kc, :], hb[:, kc, t0:t0 + n], kc == 0, kc == 7,
                           r=[("Wr", c), Hk(kc, ti)], w=[PK(pb)], mk=[Hk(kc, ti)])
                    act(prec[:, 3 + t0:3 + t0 + n], ps[pb][:, :n], AF.Copy, r=[PK(pb)], w=[("prec", q_, ti)])
                    pr = [("prec", q_, ti), ("prec_h", q_)] + ([("prec", q_, ti - 1)] if ti > 0 else [])
                    act(xc[:, :n], prec[:, 3 + t0:3 + t0 + n], AF.Identity, r=pr, w=[K_("xc")],
                        bias=small[:, 16 + c:17 + c], scale=small[:, 12 + c:13 + c])
                    for j in range(3):
                        stt(xc[:, :n], prec[:, t0 + j:t0 + j + n], small[:, 4 * j + c:4 * j + c + 1], xc[:, :n],
                            ALU.mult, ALU.add, r=pr + [K_("xc")], w=[K_("xc")])
                    act(xcb[:, :n], xc[:, :n], AF.Copy, r=[K_("xc")], w=[K_("xcb")])
                    yield
                    g0, g1 = pbase + 2, pbase + 3
                    mm(ps[g0][:, :n], Wa[:, c, :], xcb[:, :n], True, True, r=["Wa", K_("xcb")], w=[PK(g0)])
                    mm(ps[g1][:, :n], Wx[:, c, :], xcb[:, :n], True, True, r=["Wx", K_("xcb")], w=[PK(g1)])
                    act(rg[:, :n], ps[g0][:, :n], AF.Tanh, r=[PK(g0), "hprm"], w=[K_("rg")],
                        bias=hprm[:, c:c + 1], scale=0.5)
                    act(ig[:, :n], ps[g1][:, :n], AF.Tanh, r=[PK(g1), "hprm"], w=[K_("ig")],
                        bias=hprm[:, 4 + c:5 + c], scale=0.5)
                    act(aa[:, :n], rg[:, :n], AF.Exp, r=[K_("rg"), "hprm"], w=[K_("aa")],
                        scale=hprm[:, 8 + c:9 + c], bias=hprm[:, 8 + c:9 + c])
                    yield
                    pb = pbase + pcnt % 2
                    pcnt += 1
                    for kc in range(8):
                        mm(ps[pb][:, :n], Wg[c][:, kc, :], hb[:, kc, t0:t0 + n], kc == 0, kc == 7,
                           r=[("Wg", c), Hk(kc, ti)], w=[PK(pb)], mk=[Hk(kc, ti)])
                    act(xg[:, :n], ps[pb][:, :n], AF.Copy, r=[PK(pb)], w=[K_("xg")], scale=0.5)
                    act(ug[:, :n], ps[pb][:, :n], AF.Square, r=[PK(pb)], w=[K_("ug")])
                    tt("dve", mu[:, :n], aa[:, :n], aa[:, :n], ALU.mult, r=[K_("aa")], w=[K_("mu")])
                    act(mu[:, :n], mu[:, :n], AF.Sqrt, r=[K_("mu")], w=[K_("mu")], bias=0.25, scale=-0.25)
                    stt(bb[:, :n], ig[:, :n], 1.0, xc[:, :n], ALU.add, ALU.mult, r=[K_("ig"), K_("xc")],
                        w=[K_("bb")])
                    yield
                    tt("dve", bb[:, :n], bb[:, :n], mu[:, :n], ALU.mult, r=[K_("bb"), K_("mu")], w=[K_("bb")])
                    hc = tm["hs%d" % (hcnt % 2)]
                    hk = ("hs", q_, hcnt % 2)
                    init = 0.0 if prev is None else prev[0]
                    P.op("dve", lambda e, o=hc[:, :n], a_=aa[:, :n], b_=bb[:, :n], i_=init:
                         e.tensor_tensor_scan(out=o, data0=a_, data1=b_, initial=i_, op0=ALU.mult, op1=ALU.add),
                         r=[K_("aa"), K_("bb")] + ([prev[1]] if prev is not None else []), w=[hk])
                    prev = (hc[:, n - 1:n], hk)
                    hcnt += 1
                    ts("dve", ug[:, :n], ug[:, :n], 0.044715, 1.0, ALU.mult, ALU.add, r=[K_("ug")], w=[K_("ug")])
                    tt("dve", ug[:, :n], ug[:, :n], xg[:, :n], ALU.mult, r=[K_("ug"), K_("xg")], w=[K_("ug")])
                    act(ug[:, :n], ug[:, :n], AF.Tanh, r=[K_("ug")], w=[K_("ug")],
                        scale=2.0 * math.sqrt(2.0 / math.pi))
                    yield
                    stt(ug[:, :n], ug[:, :n], 1.0, xg[:, :n], ALU.add, ALU.mult, r=[K_("ug"), K_("xg")],
                        w=[K_("ug")])
                    tt("dve", yrT[:, c, t0:t0 + n], ug[:, :n], hc[:, :n], ALU.mult, r=[K_("ug"), hk],
                       w=[("yrT", c, ti)])
                    yield

        g0_, g1_ = stream(0, (0, 2)), stream(1, (1, 3))
        live = [g0_, g1_]
        if loadg is not None:
            for _ in range(5):
                step(loadg)
            live.append(loadg)
        while live:
            for g_ in list(live):
                if not step(g_):
                    live.remove(g_)
        pcnt = 0
        for ti, (t0, n) in enumerate(TILES):
            for dc in range(8):
                pb = pcnt % 8
                pcnt += 1
                for c in range(4):
                    mm(ps[pb][:, :n], Wo[:, c, dc * 128:(dc + 1) * 128], yrT[:, c, t0:t0 + n], c == 0, c == 3,
                       r=["Wo", ("yrT", c, ti)], w=[PK(pb)])
                tt("dve", R[:, dc, t0:t0 + n], R[:, dc, t0:t0 + n], ps[pb][:, :n], ALU.add,
                   r=[Rk(dc, ti), PK(pb)], w=[Rk(dc, ti)])
        A.top = SB_TOP
        P.barrier()

    def phase_mla():
        A.reset()
        Wq = A.alloc("Wq", [128, 8, 416], BF16)
        Wsw = A.alloc("Wsw", [128, 8, 96], BF16)
        Wuq = A.alloc("Wuq", [128, 2, 8, 96], BF16)
        Wuqs = A.alloc("Wuqs", [128, 2, 8, 96], BF16)
        Wukv = A.alloc("Wukv", [128, 1024], BF16)
        Wo = A.alloc("Wo", [128, 4, 1024], BF16)
        cos0 = A.alloc("cos0", [128, T], F32)
        sin0 = A.alloc("sin0", [128, T], F32)
        qnT = A.alloc("qnT", [128, 2, T], BF16)
        kvnT = A.alloc("kvnT", [128, T], BF16)
        kpe = A.alloc("kpe", [128, T], BF16)
        yaT = A.alloc("yaT", [128, 4, T], BF16)
        KT_ = [A.alloc("KT", [128, T], BF16) for _ in range(2)]
        Vh_ = [A.alloc("Vh", [128, 17, 128], BF16) for _ in range(2)]
        onesf = A.alloc("onesf", [128, 128], F32)
        zb = A.alloc("zb", [128, 128], BF16)
        WARM = 0
        t1 = A.alloc("t1", [128, 512], F32)
        t2 = A.alloc("t2", [128, 512], F32)
        mark_pq = A.cur
        pq32 = A.alloc("pq32", [128, 3, 512], F32)
        sq = A.alloc("sq", [128, 3, 512], BF16)
        rs = A.alloc("rs", [128, 512], F32)
        wiv = ev_w_in.rearrange("(kc p) n -> p kc n", p=128)
        SC = 96.0 ** -0.5
        dma("pool", Wq[:, :, :], wiv[:, :, 1024:1440], r=[], w=["Wq"])
        dma("pool", Wuq[:, :, :, :], ev_w_uq.rearrange("(kc p) (h e) -> p kc h e", p=128, e=96), r=[], w=["Wuq"])
        dma("pool", Wukv[:, :], ev_w_ukv, r=[], w=["Wukv"])
        dma("pool", Wo[:, :, :], ev_w_out[512:1024, :].rearrange("(c p) n -> p c n", p=128), r=[], w=["Wo"])
        dma("sp", cos0[64:96, :], c_rope0[0], r=[], w=["cos0"])
        dma("sp", sin0[64:96, :], c_rope0[1], r=[], w=["sin0"])
        P.op("dve", lambda e: e.memset(Wsw[:, :, :], 0.0), r=[], w=["Wsw"])
        P.op("dve", lambda e: e.memset(Wuqs[:, :, :, :], 0.0), r=[], w=["Wuqs"])
        act(Wsw[:, :, 64:80], Wq[:, :, 400:416], AF.Copy, r=["Wq", "Wsw"], w=["Wsw"], scale=-1.0)
        act(Wsw[:, :, 80:96], Wq[:, :, 384:400], AF.Copy, r=["Wq", "Wsw"], w=["Wsw"])
        for kc in range(2):
            act(Wuqs[:, kc, :, 64:80], Wuq[:, kc, :, 80:96], AF.Copy, r=["Wuq", "Wuqs"], w=["Wuqs"], scale=-1.0)
            act(Wuqs[:, kc, :, 80:96], Wuq[:, kc, :, 64:80], AF.Copy, r=["Wuq", "Wuqs"], w=["Wuqs"])
        P.op("dve", lambda e: e.memset(Vh_[0][:, :, 64:65], 1.0), r=[], w=["Vh1"])
        P.op("dve", lambda e: e.memset(Vh_[1][:, :, 0:64], 0.0), r=[], w=["Vh1"])
        P.op("dve", lambda e: e.memset(Vh_[1][:, :, 0:1], 1.0), r=["Vh1"], w=["Vh1"])
        P.op("dve", lambda e: e.memset(onesf[:, :], 1.0), r=[], w=["onesf"])
        P.op("dve", lambda e: e.memset(zb[:, :], 0.0), r=[], w=["zb"])
        pcnt = 0
        for ti, (t0, n) in enumerate(TILES):
            for c in range(3):
                pb = pcnt % 2
                pcnt += 1
                for kc in range(8):
                    mm(ps[pb][:, :n], Wq[:, kc, c * 128:(c + 1) * 128], hb[:, kc, t0:t0 + n], kc == 0, kc == 7,
                       r=["Wq", Hk(kc, ti)], w=[PK(pb)], mk=[Hk(kc, ti)])
                act(pq32[:, c, :n], ps[pb][:, :n], AF.Copy, r=[PK(pb)], w=[("pq32", c)])
                act(sq[:, c, :n], ps[pb][:, :n], AF.Square, r=[PK(pb)], w=[("sq", c)])
            for grp, (c0, c1, dd) in enumerate(((0, 2, 256.0), (2, 3, 128.0))):
                pb = 2 + grp
                for c in range(c0, c1):
                    mm(ps[pb][:, :n], onesb[:, :], sq[:, c, :n], c == c0, c == c1 - 1, r=[("sq", c), "onesb"],
                       w=[PK(pb)])
                act(rs[:, :n], ps[pb][:, :n], AF.Sqrt, r=[PK(pb), "epst"], w=["rs"], bias=epst[:, :], scale=1.0 / dd)
                recip(rs[:, :n], rs[:, :n], r=["rs"], w=["rs"])
                for c in range(c0, c1):
                    dst = qnT[:, c, t0:t0 + n] if c < 2 else kvnT[:, t0:t0 + n]
                    stt(dst, pq32[:, c, :n], small[:, 32 + c:33 + c], rs[:, :n], ALU.mult, ALU.mult,
                        r=[("pq32", c), "rs"], w=[("lat", c, ti)])
            for kc in range(8):
                mm(ps[4][0:96, :n], Wq[:, kc, 320:416], hb[:, kc, t0:t0 + n], kc == 0, kc == 7, r=["Wq", Hk(kc, ti)],
                   w=[PK(4)])
            for kc in range(8):
                mm(ps[5][0:96, :n], Wsw[:, kc, :], hb[:, kc, t0:t0 + n], kc == 0, kc == 7, r=["Wsw", Hk(kc, ti)],
                   w=[PK(5)])
            tt("dve", t1[64:96, :n], ps[4][64:96, :n], cos0[64:96, t0:t0 + n], ALU.mult, r=[PK(4), "cos0"], w=["t1"])
            tt("dve", t2[64:96, :n], ps[5][64:96, :n], sin0[64:96, t0:t0 + n], ALU.mult, r=[PK(5), "sin0"], w=["t2"])
            tt("dve", kpe[64:96, t0:t0 + n], t1[64:96, :n], t2[64:96, :n], ALU.add, r=["t1", "t2"], w=[("kpe", ti)])
        P.barrier()
        A.cur = mark_pq
        QT = [A.alloc("QT", [128, 512], BF16) for _ in range(2)]
        PTs = [A.alloc("PTs", [128, 512], BF16) for _ in range(4)]
        recr = A.alloc("recr", [128, 512], F32)
        rhb = A.alloc("rhb", [128, 512], BF16)
        rlb = A.alloc("rlb", [128, 512], BF16)
        bcs = A.alloc("bcs", [128, 512], F32)
        qcnt = 0
        ptc = 0
        scnt = 0
        ocnt = [0]
        norm_pending = []
        for h in range(8):
            hp = h % 2
            KT, Vh = KT_[hp], Vh_[hp]
            vo = 0 if hp == 0 else 64
            srow = 64 if hp == 0 else 0
            MV = 65 if hp == 0 else 128
            fcy = h // 2
            act(KT[64:96, :], kpe[64:96, :], AF.Copy, r=[("kpe", ti) for ti in range(5)], w=[("KTpe", hp)])
            for ti, (t0, n) in enumerate(TILES):
                pb = pcnt % 2
                pcnt += 1
                mm(ps[pb][0:64, :n], Wukv[:, h * 128:h * 128 + 64], kvnT[:, t0:t0 + n], True, True,
                   r=["Wukv", ("lat", 2, ti)], w=[PK(pb)])
                act(KT[0:64, t0:t0 + n], ps[pb][0:64, :n], AF.Copy, r=[PK(pb)], w=[("KT", hp, ti)])
            for grp in range(3):
                blks = [0] if grp == 0 else list(range(1 + 8 * (grp - 1), 1 + 8 * grp))
                pb = pcnt % 2
                pcnt += 1
                for i, bi in enumerate(blks):
                    b0, bn = BLOCKS[bi]
                    mm(ps[pb][:bn, i * 64:(i + 1) * 64], kvnT[:, b0:b0 + bn], Wukv[:, h * 128 + 64:h * 128 + 128],
                       True, True, r=["Wukv", ("lat", 2, tile_of_block(bi))], w=[PK(pb)])
                if grp == 0:
                    act(Vh[:16, 0, vo:vo + 64], ps[pb][:16, 0:64], AF.Copy, r=[PK(pb), "Vh1"], w=[("Vh", hp, 0)])
                else:
                    act(Vh[:, blks[0]:blks[0] + 8, vo:vo + 64],
                        ps[pb][:, :].rearrange("p (b e) -> p b e", b=8), AF.Copy, r=[PK(pb), "Vh1"],
                        w=[("Vh", hp, grp)])
            vkeys = [("Vh", hp, 0), ("Vh", hp, 1), ("Vh", hp, 2)]
            def q_prep(ti):
                nonlocal qcnt
                t0, n = TILES[ti]
                qt = QT[qcnt % 2]
                qk = ("QT", qcnt % 2)
                qcnt += 1
                for kc in range(2):
                    mm(ps[2][0:96, :n], Wuq[:, kc, h, :], qnT[:, kc, t0:t0 + n], kc == 0, kc == 1,
                       r=["Wuq", ("lat", kc, ti)], w=[PK(2)])
                for kc in range(2):
                    mm(ps[3][0:96, :n], Wuqs[:, kc, h, :], qnT[:, kc, t0:t0 + n], kc == 0, kc == 1,
                       r=["Wuqs", ("lat", kc, ti)], w=[PK(3)])
                act(qt[0:64, :n], ps[2][0:64, :n], AF.Copy, r=[PK(2)], w=[qk])
                tt("dve", t1[64:96, :n], ps[2][64:96, :n], cos0[64:96, t0:t0 + n], ALU.mult, r=[PK(2), "cos0"],
                   w=["t1"])
                tt("dve", t2[64:96, :n], ps[3][64:96, :n], sin0[64:96, t0:t0 + n], ALU.mult, r=[PK(3), "sin0"],
                   w=["t2"])
                tt("dve", qt[64:96, :n], t1[64:96, :n], t2[64:96, :n], ALU.add, r=["t1", "t2"], w=[qk])
                return qt, qk

            nxt_q = q_prep(0)
            for ti, (t0, n) in enumerate(TILES):
                qt, qk = nxt_q
                qblocks = [bi for bi in range(17) if tile_of_block(bi) == ti]
                ob = 6 + ocnt[0] % 2
                ocnt[0] += 1
                O = ps[ob]
                kbs = list(range(0, qblocks[-1] + 1))

                def scores(kb):
                    nonlocal scnt, ptc
                    ks, kn = BLOCKS[kb]
                    c0 = max(ks, t0) - t0
                    sb = 4 + scnt % 2
                    scnt += 1
                    mm(ps[sb][:kn, c0:n], KT[0:96, ks:ks + kn], qt[0:96, c0:n], True, True,
                       r=[("KTpe", hp), ("KT", hp, tile_of_block(kb)), qk], w=[PK(sb)], mk=[qk])
                    pt = PTs[ptc % 4]
                    pk = ("PTs", ptc % 4)
                    ptc += 1
                    act(pt[:kn, c0:n], ps[sb][:kn, c0:n], AF.Exp, r=[PK(sb)], w=[pk], scale=SC)
                    if ks >= t0:
                        tt("dve", pt[:kn, c0:c0 + kn], pt[:kn, c0:c0 + kn], maskb[:kn, :kn], ALU.mult,
                           r=[pk, "maskb"], w=[pk])
                    return pt, pk

                def pv(kb, pt, pk):
                    ks, kn = BLOCKS[kb]
                    c0 = max(ks, t0) - t0
                    mm(O[0:MV, c0:n], Vh[:kn, kb, 0:MV], pt[:kn, c0:n], kb == 0, kb == kbs[-1],
                       r=[pk] + vkeys, w=[PK(ob)], mk=[pk])
                    if WARM and kb != kbs[-1] and n == 512:
                        for _ in range(WARM):
                            mm(O[0:MV, 0:n], zb[:, 0:MV], kvnT[:, 0:n], False, False, r=["zb"], w=[PK(ob)])

                pend = []
                for ki, kb in enumerate(kbs):
                    pend.append((kb,) + scores(kb))
                    if len(pend) > 1:
                        pv(*pend.pop(0))
                    if ki == 2 and norm_pending:
                        norm_pending.pop(0)()
                while pend:
                    pv(*pend.pop(0))
                if norm_pending:
                    norm_pending.pop(0)()
                if ti + 1 < 5:
                    nxt_q = q_prep(ti + 1)

                def normalize(ob=ob, O=O, n=n, t0=t0, ti=ti, vo=vo, srow=srow, fcy=fcy, hp=hp):
                    nonlocal pcnt
                    recip(recr[srow:srow + 1, :n], O[srow:srow + 1, :n], r=[PK(ob)], w=["recr"])
                    P.op("dve", lambda e: e.tensor_copy(out=rhb[srow:srow + 1, :n], in_=recr[srow:srow + 1, :n]),
                         r=["recr"], w=["rhb"])
                    tt("dve", rlb[srow:srow + 1, :n], recr[srow:srow + 1, :n], rhb[srow:srow + 1, :n], ALU.subtract,
                       r=["recr", "rhb"], w=["rlb"])
                    bb_ = pcnt % 2
                    pcnt += 1
                    mm(ps[bb_][0:vo + 64, :n], onesb[srow:srow + 1, 0:vo + 64], rhb[srow:srow + 1, :n], True, False,
                       r=["onesb", "rhb"], w=[PK(bb_)])
                    mm(ps[bb_][0:vo + 64, :n], onesb[srow:srow + 1, 0:vo + 64], rlb[srow:srow + 1, :n], False, True,
                       r=["onesb", "rlb"], w=[PK(bb_)])
                    act(bcs[vo:vo + 64, :n], ps[bb_][vo:vo + 64, :n], AF.Copy, r=[PK(bb_)], w=["bcs"])
                    tt("dve", yaT[vo:vo + 64, fcy, t0:t0 + n], O[vo:vo + 64, :n], bcs[vo:vo + 64, :n], ALU.mult,
                       r=[PK(ob), "bcs"], w=[("yaT", fcy, ti, hp)])

                norm_pending.append(normalize)
        while norm_pending:
            norm_pending.pop(0)()
        for ti, (t0, n) in enumerate(TILES):
            for dc in range(8):
                pb = pcnt % 6
                pcnt += 1
                for fc in range(4):
                    mm(ps[pb][:, :n], Wo[:, fc, dc * 128:(dc + 1) * 128], yaT[:, fc, t0:t0 + n], fc == 0, fc == 3,
                       r=["Wo", ("yaT", fc, ti, 0), ("yaT", fc, ti, 1)], w=[PK(pb)])
                tt("dve", R[:, dc, t0:t0 + n], R[:, dc, t0:t0 + n], ps[pb][:, :n], ALU.add,
                   r=[Rk(dc, ti), PK(pb)], w=[Rk(dc, ti)])
        P.barrier()

    for s in range(1 if 'oneseq' in KVAR else SEQ_PER_CORE):
        if stage in ("full", "l0", "lru"):
            phase_lru(load_gen(s))
        elif 'noload' not in KVAR:
            phase_load(s)
        if stage in ("full", "l0", "mla"):
            phase_mla()
        if stage in ("full", "l0"):
            seq_ln_mlp_ln(0, 0, 1, False)
        if stage in ("full", "ret"):
            phase_ret()
        if stage in ("full",):
            seq_ln_mlp_ln(1, 2, 3, True)
        if stage in ("mlp",):
            phase_mlp(1, end_barrier=False)
        if stage in ("mlp", "ret", "lru", "mla", "l0"):
            phase_ln(3, final=True)
        if 'nostore' not in KVAR:
            phase_store(s)
    P.emit()
    nc._in_names = in_names
    return nc, in_names


def make_consts():
    c = {}
    c["c_ident"] = np.eye(128, dtype=np.float32)
    k = np.arange(128)
    c["c_mask"] = (k[None, :] >= k[:, None]).astype(np.float32)
    pos = np.arange(T, dtype=np.float32)
    inv0 = (10000.0 ** (-np.arange(16, dtype=np.float32) / 16)).astype(np.float32)
    ang0 = pos[None, :] * inv0[np.arange(32) % 16][:, None]
    c["c_rope0"] = np.stack([np.cos(ang0), np.sin(ang0)]).astype(np.float32)
    inv1 = (10000.0 ** (-np.arange(128, dtype=np.float32) / 128)).astype(np.float32)
    ang1 = pos[None, :] * inv1[:, None]
    c["c_rope1"] = np.stack([np.cos(ang1), np.sin(ang1)]).astype(np.float32)
    gam = np.array([1.0 - 2.0 ** (-5.0 - h) for h in range(4)], dtype=np.float64)
    m = np.arange(512)
    diff = (m[None, :] - k[:, None]).astype(np.float64)
    dm = np.where(diff >= 0, gam[:, None, None] ** np.maximum(diff, 0.0)[None], 0.0) / 16.0
    c["c_dm"] = dm.astype(np.float32)
    i512 = (np.arange(512) + 1).astype(np.float64)
    c["c_qdec"] = np.broadcast_to((gam[:, None] ** i512[None, :])[:, None, :], (4, 128, 512)).astype(np.float32).copy()
    kd = np.zeros((128, 20), dtype=np.float64)
    for h in range(4):
        for b_ in range(4):
            kd[:, h * 5 + b_] = gam[h] ** (511.0 - (128.0 * b_ + k)) / 16.0
        kd[:, h * 5 + 4] = gam[h] ** np.maximum(15.0 - k, 0.0) / 16.0
    c["c_kdec"] = kd.astype(np.float32)
    return c


def pack_small(inputs):
    g = lambda k: np.asarray(inputs[k], dtype=np.float32)
    cols = []
    cw = g("ev_conv_w")[0].reshape(4, 4, 128)
    cols.append(cw.transpose(2, 0, 1).reshape(128, 16))
    for k in ("ev_conv_b", "ev_b_rg_a", "ev_b_rg_x", "ev_lru_lambda"):
        cols.append(g(k)[0].reshape(4, 128).T)
    cols.append(g("ev_q_norm_g")[0].reshape(2, 128).T)
    cols.append(g("ev_kv_norm_g")[0].reshape(1, 128).T)
    for kind in ("g", "b"):
        for li in range(4):
            src = g(("ln_mix_" if li % 2 == 0 else "ln_mlp_") + kind)[li // 2]
            cols.append(src.reshape(8, 128).T)
    return np.ascontiguousarray(np.concatenate(cols, axis=1), dtype=np.float32)


_NC_CACHE = {}


def run(inputs, stage="full", trace=False):
    if stage not in _NC_CACHE:
        _NC_CACHE[stage] = build(stage)
    nc, in_names = _NC_CACHE[stage]
    consts = make_consts()
    f = lambda a: np.ascontiguousarray(np.asarray(a, dtype=np.float32))
    shared = {
        "meta_tokens": f(inputs["meta_tokens"]),
        "ev_w_in": f(inputs["ev_w_in"][0]), "ev_w_rg_a": f(inputs["ev_w_rg_a"][0]), "ev_w_rg_x": f(inputs["ev_w_rg_x"][0]), "ev_w_uq": f(inputs["ev_w_uq"][0]), "ev_w_ukv": f(inputs["ev_w_ukv"][0]),
        "ev_w_out": f(inputs["ev_w_out"][0]), "od_w_in": f(inputs["od_w_in"][0]),
        "od_w_out": f(inputs["od_w_out"][0]),
        "smallp": pack_small(inputs),
        "mlp_w1": f(inputs["mlp_w1"]), "mlp_w2": f(inputs["mlp_w2"]),
    }
    shared.update(consts)
    x = f(inputs["x"])
    in_maps = []
    for c in range(NCORES):
        m = {k: v for k, v in shared.items() if k in in_names}
        m["x"] = x[c * SEQ_PER_CORE:(c + 1) * SEQ_PER_CORE]
        in_maps.append(m)
    res = run_bass_kernel_spmd(nc, in_maps, core_ids=list(range(NCORES)), trace=trace)
    out = np.concatenate([np.asarray(r["out"]) for r in res.results], axis=0)
    return out.astype(np.float32), res


def kernel(**inputs):
    out, _ = run(inputs, "full")
    return out
```

```python
import math
import numpy as np
import concourse.bass as bass
import concourse.mybir as mybir
from concourse.bass_utils import run_bass_kernel_spmd

F32 = mybir.dt.float32
BF16 = mybir.dt.bfloat16
AF = mybir.ActivationFunctionType
ALU = mybir.AluOpType

NCORES = 8
SEQ_PER_CORE = 2
D = 1024
T = 2064
NMETA = 16
ALPHA = 4.0 ** 0.25
EPS = 1e-5
TILES = [(0, 16)] + [(16 + 512 * i, 512) for i in range(4)]
BLOCKS = [(0, 16)] + [(16 + 128 * j, 128) for j in range(16)]
GEN = 3000
FUSE_WAIT = True


def tile_of_block(bi):
    return 0 if bi == 0 else 1 + (bi - 1) // 4


class Op:
    __slots__ = ("eng", "fn", "deps", "dma", "semkey", "signal", "cnt", "dcount", "clock", "needed", "rdeps", "fused")


class Prog:
    ENGS = ("pe", "act", "dve", "pool", "sp")

    def __init__(self, nc):
        self.nc = nc
        self.q = {e: [] for e in self.ENGS}
        self.lastw = {}
        self.readers = {}
        self.dma_counts = {}
        self.last_op = {e: None for e in self.ENGS}
        self.pending_barrier = {e: [] for e in self.ENGS}
        self.all_ops = []
        self.dma_ops = {}

    def op(self, eng, fn, r=(), w=(), dma=False, semkey=None):
        o = Op()
        o.eng, o.fn, o.dma, o.signal = eng, fn, dma, False
        o.cnt = 0
        deps = {}

        def add(d, raw):
            if d.dma:
                deps[("d", d.semkey)] = ("d", d.semkey, self.dma_counts[d.semkey])
                return
            if d.eng == eng and not dma:
                if eng == "pe":
                    return
            deps[("c", id(d))] = ("c", d)
            d.signal = True

        for k in r:
            lw = self.lastw.get(k)
            if lw is not None:
                add(lw, True)
            if isinstance(k, tuple) and k[0] == "ps":
                for rd in self.readers.get(k, {}).values():
                    if rd.eng != eng:
                        add(rd, False)
        rdeps = set(deps.keys())
        for k in w:
            lw = self.lastw.get(k)
            if lw is not None:
                add(lw, False)
            for rd in self.readers.get(k, {}).values():
                add(rd, False)
        for dep in self.pending_barrier[eng]:
            if dep[0] == "c":
                deps[("c", id(dep[1]))] = dep
            else:
                old_ = deps.get(("d", dep[1]))
                if old_ is None or old_[2] < dep[2]:
                    deps[("d", dep[1])] = dep
            rdeps.add(("c", id(dep[1])) if dep[0] == "c" else ("d", dep[1]))
        self.pending_barrier[eng] = []
        o.deps = list(deps.values())
        o.rdeps = rdeps
        if dma:
            o.semkey = semkey if semkey is not None else w[0]
            self.dma_counts[o.semkey] = self.dma_counts.get(o.semkey, 0) + 1
            o.dcount = self.dma_counts[o.semkey]
            self.dma_ops[(o.semkey, o.dcount)] = o
        for k in r:
            self.readers.setdefault(k, {})[("d", id(o)) if dma else eng] = o
        for k in w:
            self.lastw[k] = o
            self.readers[k] = {}
        self.q[eng].append(o)
        self.all_ops.append(o)
        if not dma:
            self.last_op[eng] = o
        return o

    def barrier(self, keep_prefixes=()):
        deps = []
        for e in self.ENGS:
            lo = self.last_op[e]
            if lo is not None:
                lo.signal = True
                deps.append(("c", lo))
        for k, c in self.dma_counts.items():
            deps.append(("d", k, c))
        for e in self.ENGS:
            self.pending_barrier[e] = [d for d in deps if not (d[0] == "c" and d[1].eng == e and e == "pe")] + \
                self.pending_barrier[e]
        self.lastw = {}
        self.readers = {}

    def emit(self):
        nc = self.nc
        csem = {}
        dsem = {}

        def get_csem(e, g):
            if (e, g) not in csem:
                csem[(e, g)] = nc.alloc_semaphore(f"c_{e}_{g}")
            return csem[(e, g)]

        def get_dsem(k):
            if k not in dsem:
                dsem[k] = nc.alloc_semaphore("d_" + "_".join(str(x) for x in (k if isinstance(k, tuple) else (k,))))
            return dsem[k]

        for e in self.ENGS:
            c = 0
            for o in self.q[e]:
                if o.signal and not o.dma:
                    c += 1
                    o.cnt = c

        known = {e: {} for e in self.ENGS}

        def merge(dst, src):
            for k_, v_ in src.items():
                if dst.get(k_, 0) < v_:
                    dst[k_] = v_

        def dkey(dep):
            return ("c", id(dep[1])) if dep[0] == "c" else ("d", dep[1])

        def info(dep):
            if dep[0] == "c":
                return ("c", dep[1].eng), dep[1].cnt, dep[1]
            return ("d", dep[1]), dep[2], self.dma_ops.get((dep[1], dep[2]))

        for o in self.all_ops:
            kn = known[o.eng]
            needed = [dep for dep in o.deps if kn.get(info(dep)[0], 0) < info(dep)[1]]
            fused = None
            if o.eng == "pe" and not o.dma:
                cand = [dep for dep in needed if dkey(dep) not in o.rdeps]
                if cand:
                    fused = cand[-1]
            keep = []
            for dep in needed:
                if dep is fused:
                    continue
                key, val, src = info(dep)
                if kn.get(key, 0) >= val:
                    continue
                keep.append(dep)
                kn[key] = val
                if src is not None and getattr(src, "clock", None) is not None:
                    merge(kn, src.clock)
            o.needed = keep
            o.fused = fused
            o.clock = dict(kn)
            if fused is not None:
                key, val, src = info(fused)
                if o.clock.get(key, 0) < val:
                    o.clock[key] = val
                if src is not None and getattr(src, "clock", None) is not None:
                    merge(o.clock, src.clock)

        def run(e):
            def body(engine):
                waited = {}
                for o in self.q[e]:
                    need = []
                    for dep in o.needed:
                        if dep[0] == "c":
                            d = dep[1]
                            g = (d.cnt - 1) // GEN
                            sem = get_csem(d.eng, g)
                            key = ("c", d.eng, g)
                            val = d.cnt - g * GEN
                        else:
                            sem = get_dsem(dep[1])
                            key = ("d", dep[1])
                            val = 16 * dep[2]
                        if waited.get(key, 0) < val:
                            need.append((sem, val))
                            waited[key] = val
                    if e == "pe" and getattr(o, "fused", None) is not None:
                        dep = o.fused
                        if dep[0] == "c":
                            d = dep[1]
                            g = (d.cnt - 1) // GEN
                            fsem, fval = get_csem(d.eng, g), d.cnt - g * GEN
                        else:
                            fsem, fval = get_dsem(dep[1]), 16 * dep[2]
                        for sem, val in need:
                            engine.wait_ge(sem, val)
                        ins = o.fn(engine)
                        ins._wait_ge(fsem, fval)
                        if o.signal:
                            g = (o.cnt - 1) // GEN
                            ins.then_inc(get_csem(e, g), 1)
                        continue
                    fuse = FUSE_WAIT and (not o.dma) and e in ("act", "dve", "pool") and len(need) > 0
                    for sem, val in (need[:-1] if fuse else need):
                        engine.wait_ge(sem, val)
                    ins = o.fn(engine)
                    if fuse:
                        ins._wait_ge(need[-1][0], need[-1][1])
                    if o.dma:
                        ins.then_inc(get_dsem(o.semkey), 16)
                    elif o.signal:
                        g = (o.cnt - 1) // GEN
                        ins.then_inc(get_csem(e, g), 1)
                if e == "sp":
                    for k, cnt in self.dma_counts.items():
                        engine.wait_ge(get_dsem(k), 16 * cnt)
            return body

        with nc.Block() as block:
            block.sync(run("sp"))
            block.gpsimd(run("pool"))
            block.scalar(run("act"))
            block.vector(run("dve"))
            block.tensor(run("pe"))


class Arena:
    def __init__(self, nc, base, top):
        self.nc, self.base, self.top, self.cur = nc, base, top, base
        self.n = 0

    def reset(self):
        self.cur = self.base

    def alloc(self, name, shape, dtype):
        esz = 4 if dtype == F32 else 2
        per = esz
        for s in shape[1:]:
            per *= s
        off = (self.cur + 63) // 64 * 64
        assert off + per <= self.top, f"arena overflow {name}: need {per} at {off}, top {self.top}"
        self.cur = off + per
        self.n += 1
        return self.nc.alloc_sbuf_tensor_at(f"{name}_{self.n}", list(shape), dtype, offset=off)


def build(stage="full"):
    import os
    KVAR = os.environ.get('KVAR', '')
    nc = bass.Bass("TRN2", target_bir_lowering=False)
    P = Prog(nc)

    USE = {
        "ev": ("full", "l0", "lru", "mla"), "od": ("full", "ret"), "mlp": ("full", "l0", "mlp"),
    }
    in_names = []

    def din(name, shape):
        grp = name.split("_")[0]
        if grp in USE and stage not in USE[grp]:
            return None
        in_names.append(name)
        return nc.dram_tensor(name, list(shape), F32, kind="ExternalInput").ap()

    x_d = din("x", [SEQ_PER_CORE, 2048, D])
    meta_d = din("meta_tokens", [16, D])
    ev_w_in = din("ev_w_in", [D, 1440])
    ev_w_rg_a = din("ev_w_rg_a", [4, 128, 128])
    ev_w_rg_x = din("ev_w_rg_x", [4, 128, 128])
    ev_w_uq = din("ev_w_uq", [256, 768])
    ev_w_ukv = din("ev_w_ukv", [128, 1024])
    ev_w_out = din("ev_w_out", [1024, 1024])
    od_w_in = din("od_w_in", [D, 6144])
    od_w_out = din("od_w_out", [2048, D])
    smallp_d = din("smallp", [128, 99])
    mlp_w1 = din("mlp_w1", [2, D, 4096])
    mlp_w2 = din("mlp_w2", [2, 4096, D])
    c_ident = din("c_ident", [128, 128])
    c_mask = din("c_mask", [128, 128])
    c_rope0 = din("c_rope0", [2, 32, T])
    c_rope1 = din("c_rope1", [2, 128, T])
    c_dm = din("c_dm", [4, 128, 512])
    c_qdec = din("c_qdec", [4, 128, 512])
    c_kdec = din("c_kdec", [128, 20])
    out_d = nc.dram_tensor("out", [SEQ_PER_CORE, 2048, D], F32, kind="ExternalOutput").ap()

    GAM = [1.0 - 2.0 ** (-5.0 - h) for h in range(4)]

    SB_BASE, SB_TOP = 16512, 229344
    fixed = Arena(nc, SB_BASE, SB_TOP)
    R = fixed.alloc("R", [128, 8, T], F32)
    hb = fixed.alloc("hb", [128, 8, T], BF16)
    ident = fixed.alloc("ident", [128, 128], F32)
    identb = fixed.alloc("identb", [128, 128], BF16)
    onesb = fixed.alloc("onesb", [128, 128], BF16)
    maskb = fixed.alloc("maskb", [128, 128], BF16)
    maskf = fixed.alloc("maskf", [128, 128], F32)
    small = fixed.alloc("small", [128, 99], F32)
    lna = fixed.alloc("lna", [128, 64], F32)
    epst = fixed.alloc("epst", [128, 1], F32)
    LN_BYTES = 23040
    LNA = Arena(nc, SB_TOP - LN_BYTES, SB_TOP)
    ln_rb = LNA.alloc("rb", [128, 8, 512], BF16)
    ln_rsq = LNA.alloc("rsq", [128, 8, 512], BF16)
    ln_mean = LNA.alloc("mean", [128, 512], F32)
    ln_var = LNA.alloc("var", [128, 512], F32)
    ln_rstd = LNA.alloc("rstd", [128, 512], F32)
    A = Arena(nc, fixed.cur, SB_TOP)

    ps = [nc.alloc_psum_tensor(f"ps{i}", [128, 512], F32) for i in range(8)]

    def PK(i):
        return ("ps", i)

    def mm(out, lhsT, rhs, start, stop, r, w):
        return P.op("pe", lambda e: e.matmul(out, lhsT, rhs, start=start, stop=stop), r=r, w=w)

    def tr(out, in_, idn, r, w):
        return P.op("pe", lambda e: e.transpose(out, in_, idn), r=r, w=w)

    def act(out, in_, func, r, w, bias=None, scale=1.0):
        if bias is None:
            return P.op("act", lambda e: e.activation(out=out, in_=in_, func=func, scale=scale), r=r, w=w)
        return P.op("act", lambda e: e.activation(out=out, in_=in_, func=func, bias=bias, scale=scale), r=r, w=w)

    def tt(eng, out, in0, in1, op, r, w):
        return P.op(eng, lambda e: e.tensor_tensor(out=out, in0=in0, in1=in1, op=op), r=r, w=w)

    def ts(eng, out, in0, s1, s2, op0, op1, r, w):
        return P.op(eng, lambda e: e.tensor_scalar(out=out, in0=in0, scalar1=s1, scalar2=s2, op0=op0, op1=op1),
                    r=r, w=w)

    def stt(out, in0, scalar, in1, op0, op1, r, w):
        return P.op("dve", lambda e: e.scalar_tensor_tensor(out=out, in0=in0, scalar=scalar, in1=in1,
                                                            op0=op0, op1=op1), r=r, w=w)

    def dma(eng, out, in_, r, w, semkey=None):
        return P.op(eng, lambda e: e.dma_start(out=out, in_=in_), r=r, w=w, dma=True, semkey=semkey)

    def recip(out, in_, r, w):
        return P.op("dve", lambda e: e.reciprocal(out=out, in_=in_), r=r, w=w)

    def Rk(c, ti):
        return ("R", c, ti)

    def Hk(c, ti):
        return ("hb", c, ti)

    dma("sp", ident[:, :], c_ident, r=[], w=["ident"])
    dma("sp", maskf[:, :], c_mask, r=[], w=["maskf"])
    if 'noconst' not in KVAR:
        P.op("dve", lambda e: e.tensor_copy(out=identb[:, :], in_=ident[:, :]), r=["ident"], w=["identb"])
        P.op("dve", lambda e: e.tensor_copy(out=maskb[:, :], in_=maskf[:, :]), r=["maskf"], w=["maskb"])
        P.op("dve", lambda e: e.memset(onesb[:, :], 1.0), r=[], w=["onesb"])
        P.op("dve", lambda e: e.memset(epst[:, :], EPS), r=[], w=["epst"])
        dma("sp", small[:, :], smallp_d, r=[], w=["small"])
        act(lna[:, :], small[:, 35:99], AF.Copy, r=["small"], w=["lna"], scale=ALPHA)
    P.barrier()

    XA = Arena(nc, SB_TOP - 4160, SB_TOP)

    def load_gen(s):
        XA.reset()
        xin = [XA.alloc("xin", [128, D], F32)] * 2
        for bi, (t0, n) in enumerate(BLOCKS):
            if 'nometa' in KVAR and bi == 0:
                continue
            ti = tile_of_block(bi)
            src = meta_d[0:16, :] if bi == 0 else x_d[s, t0 - 16:t0 - 16 + n, :]
            xt = xin[bi % 2]
            dma("sp", xt[:n, :], src, r=[], w=[("xin", 0)])
            for half in range(2):
                pb = (bi * 2 + half) % 4
                for c4 in range(4):
                    c = half * 4 + c4
                    tr(ps[pb][:, c4 * 128:c4 * 128 + n], xt[:n, c * 128:(c + 1) * 128], ident[:n, :n],
                       r=[("xin", 0), "ident"], w=[PK(pb)])
                pv = ps[pb][:, :].rearrange("p (c t) -> p c t", c=4)[:, :, 0:n]
                act(R[:, half * 4:half * 4 + 4, t0:t0 + n], pv, AF.Copy, r=[PK(pb)],
                    w=[Rk(c, ti) for c in range(half * 4, half * 4 + 4)], scale=ALPHA)
                P.op("dve", lambda e, o=hb[:, half * 4:half * 4 + 4, t0:t0 + n], i=pv: e.tensor_copy(out=o, in_=i),
                     r=[PK(pb)], w=[Hk(c, ti) for c in range(half * 4, half * 4 + 4)])
            yield

    def phase_load(s):
        g = load_gen(s)
        while step(g):
            pass
        P.barrier()

    def phase_store(s):
        A.reset()
        xo = [A.alloc("xo", [128, D], F32) for _ in range(2)]
        for bi, (t0, n) in enumerate(BLOCKS):
            if bi == 0:
                continue
            ti = tile_of_block(bi)
            xt = xo[bi % 2]
            for half in range(2):
                pb = (bi * 2 + half) % 4
                for c4 in range(4):
                    c = half * 4 + c4
                    tr(ps[pb][:n, c4 * 128:(c4 + 1) * 128], R[:, c, t0:t0 + n], ident[:, :],
                       r=[Rk(c, ti), "ident"], w=[PK(pb)])
                if half == 0:
                    act(xt[:n, 0:512], ps[pb][:n, :], AF.Copy, r=[PK(pb)], w=[("xo", bi % 2, 0)])
                else:
                    P.op("dve", lambda e, o=xt[:n, 512:1024], i=ps[pb][:n, :]: e.tensor_copy(out=o, in_=i),
                         r=[PK(pb)], w=[("xo", bi % 2, 1)])
            dma("sp", out_d[s, t0 - 16:t0 - 16 + n, :], xt[:n, :], r=[("xo", bi % 2, 0), ("xo", bi % 2, 1)],
                w=[("outd", bi)], semkey=("st", bi % 2))
        P.barrier()

    def ln_gen(li, final=False):
        rb = [ln_rb] * 2
        rsq = [ln_rsq] * 2
        mean = [ln_mean] * 2
        var = [ln_var] * 2
        rstd = [ln_rstd] * 2
        def gcol(kind, c, scaled):
            t_ = lna if scaled else small
            o_ = (0 if scaled else 35) + kind * 32 + li * 8 + c
            return t_[:, o_:o_ + 1]
        for ti, (t0, n) in enumerate(TILES):
            b = 0
            for c in range(8):
                act(rb[b][:, c, :n], R[:, c, t0:t0 + n], AF.Copy, r=[Rk(c, ti)], w=[("rb", b, c)])
                act(rsq[b][:, c, :n], R[:, c, t0:t0 + n], AF.Square, r=[Rk(c, ti)], w=[("rsq", b, c)])
            p1, p2 = 6 + 0 * ti, 7
            for c in range(8):
                mm(ps[p1][:, :n], onesb[:, :], rb[b][:, c, :n], c == 0, c == 7, r=[("rb", b, c), "onesb"], w=[PK(p1)])
            for c in range(8):
                mm(ps[p2][:, :n], onesb[:, :], rsq[b][:, c, :n], c == 0, c == 7, r=[("rsq", b, c), "onesb"],
                   w=[PK(p2)])
            act(mean[b][:, :n], ps[p1][:, :n], AF.Copy, r=[PK(p1)], w=[("mean", b)], scale=1.0 / D)
            tt("dve", var[b][:, :n], mean[b][:, :n], mean[b][:, :n], ALU.mult, r=[("mean", b)], w=[("var", b)])
            stt(var[b][:, :n], ps[p2][:, :n], 1.0 / D, var[b][:, :n], ALU.mult, ALU.subtract,
                r=[PK(p2), ("var", b)], w=[("var", b)])
            act(rstd[b][:, :n], var[b][:, :n], AF.Sqrt, r=[("var", b), "epst"], w=[("rstd", b)], bias=epst[:, :])
            recip(rstd[b][:, :n], rstd[b][:, :n], r=[("rstd", b)], w=[("rstd", b)])
            for c in range(8):
                tt("dve", R[:, c, t0:t0 + n], R[:, c, t0:t0 + n], mean[b][:, :n], ALU.subtract,
                   r=[Rk(c, ti), ("mean", b)], w=[Rk(c, ti)])
            for c in range(8):
                tt("pool", R[:, c, t0:t0 + n], R[:, c, t0:t0 + n], rstd[b][:, :n], ALU.mult,
                   r=[Rk(c, ti), ("rstd", b)], w=[Rk(c, ti)])
            for c in range(8):
                if not final:
                    ts("dve", hb[:, c, t0:t0 + n], R[:, c, t0:t0 + n], gcol(0, c, False), gcol(1, c, False),
                       ALU.mult, ALU.add, r=[Rk(c, ti)], w=[Hk(c, ti)])
                act(R[:, c, t0:t0 + n], R[:, c, t0:t0 + n], AF.Identity, r=[Rk(c, ti)], w=[Rk(c, ti)],
                    bias=gcol(1, c, not final), scale=gcol(0, c, not final))
            yield

    def step(g):
        try:
            next(g)
            return True
        except StopIteration:
            return False

    def phase_ln(li, final=False, end_barrier=True):
        g = ln_gen(li, final)
        while step(g):
            pass
        if end_barrier:
            P.barrier()

    mlp_state = {}

    def mlp_prefetch(l):
        A.reset()
        A.top = SB_TOP - LN_BYTES
        mlp_state["pre"] = True
        phase_mlp(l, prefetch_only=True)

    def phase_mlp(l, end_barrier=True, prefetch_only=False):
        if not prefetch_only and not mlp_state.get("pre"):
            A.reset()
            A.top = SB_TOP - LN_BYTES
        if not prefetch_only and mlp_state.get("pre"):
            mlp_state["pre"] = False
            W1, W2, ub, tmpb, load = mlp_state["bufs"]
            g_ = mlp_main(l, end_barrier, W1, W2, ub, tmpb, load)
            while step(g_):
                pass
            return
        NG = 4
        W1 = [A.alloc("W1", [128, 8, 1024], BF16) for _ in range(2)]
        W2 = [A.alloc("W2", [128, 8, 1024], BF16) for _ in range(2)]
        ub = [A.alloc("ub", [128, 8, 512], BF16) for _ in range(2)]
        tmpb = [A.alloc("tmpb", [128, 512], BF16) for _ in range(3)]
        w1v = mlp_w1[l].rearrange("(kc p) n -> p kc n", p=128)
        w2v = mlp_w2[l].rearrange("(fc p) n -> p fc n", p=128)

        def load(g):
            sl = g % 2
            for hlf in range(2):
                dma("pool", W1[sl][:, 4 * hlf:4 * hlf + 4, :], w1v[:, 4 * hlf:4 * hlf + 4, g * 1024:(g + 1) * 1024],
                    r=[], w=[("W1", sl)])
            for hlf in range(2):
                dma("pool", W2[sl][:, 4 * hlf:4 * hlf + 4, :], w2v[:, g * 8 + 4 * hlf:g * 8 + 4 * hlf + 4, :],
                    r=[], w=[("W2", sl)])

        load(0)
        if prefetch_only:
            mlp_state["bufs"] = (W1, W2, ub, tmpb, load)
            return
        g_ = mlp_main(l, end_barrier, W1, W2, ub, tmpb, load, loaded0=True)
        while step(g_):
            pass

    def mlp_main(l, end_barrier, W1, W2, ub, tmpb, load, loaded0=True):
        NG = 4
        ucnt = 0
        fcnt = 0
        it = 0
        for g in range(NG):
            sl = g % 2
            if g + 1 < NG:
                load(g + 1)
            for ti, (t0, n) in enumerate(TILES):
                us = it % 2
                it += 1
                for j in range(8):
                    pb = ucnt % 2
                    tb = ucnt % 3
                    ucnt += 1
                    for kc in range(8):
                        mm(ps[pb][:, :n], W1[sl][:, kc, j * 128:(j + 1) * 128], hb[:, kc, t0:t0 + n], kc == 0, kc == 7,
                           r=[("W1", sl), Hk(kc, ti)], w=[PK(pb)])
                    act(tmpb[tb][:, :n], ps[pb][:, :n], AF.Relu, r=[PK(pb)], w=[("tmpb", tb)])
                    tt("dve", ub[us][:, j, :n], tmpb[tb][:, :n], tmpb[tb][:, :n], ALU.mult, r=[("tmpb", tb)],
                       w=[("ub", us, j)])
                for dc in range(8):
                    pb = 2 + fcnt % 4
                    fcnt += 1
                    for j in range(8):
                        mm(ps[pb][:, :n], W2[sl][:, j, dc * 128:(dc + 1) * 128], ub[us][:, j, :n], j == 0, j == 7,
                           r=[("W2", sl), ("ub", us, j)], w=[PK(pb)])
                    tt("dve", R[:, dc, t0:t0 + n], R[:, dc, t0:t0 + n], ps[pb][:, :n], ALU.add,
                       r=[Rk(dc, ti), PK(pb)], w=[Rk(dc, ti)])
                yield
        A.top = SB_TOP
        if end_barrier:
            P.barrier()

    def seq_ln_mlp_ln(l, li_a, li_b, final_b):
        mlp_prefetch(l)
        mlp_state["pre"] = False
        W1, W2, ub, tmpb, load = mlp_state["bufs"]
        ga = ln_gen(li_a, False)
        gm = mlp_main(l, False, W1, W2, ub, tmpb, load)
        gb = ln_gen(li_b, final_b)
        step(ga)
        step(ga)
        for t in range(5):
            step(gm)
            if t + 2 < 5:
                step(ga)
        for _ in range(10):
            step(gm)
        step(gm)
        step(gm)
        for t in range(5):
            step(gb)
            if t + 2 < 5:
                step(gm)
        while step(gm):
            pass
        while step(ga):
            pass
        while step(gb):
            pass
        P.barrier()

    def phase_ret():
        A.reset()
        Whqk = A.alloc("Whqk", [128, 8, 512], BF16)
        Whv = A.alloc("Whv", [128, 8, 512], BF16)
        Whg = A.alloc("Whg", [128, 8, 512], BF16)
        Woh = A.alloc("Woh", [128, 4, 1024], BF16)
        cosT = A.alloc("cosT", [128, T], F32)
        sinT = A.alloc("sinT", [128, T], F32)
        dmT = A.alloc("dmT", [128, 512], F32)
        qdec = A.alloc("qdec", [128, 512], F32)
        kdec = A.alloc("kdec", [128, 20], F32)
        qT = [A.alloc("qT", [128, 2, 512], BF16) for _ in range(2)]
        kT = [A.alloc("kT", [128, 2, 512], BF16) for _ in range(2)]
        qdT = A.alloc("qdT", [128, 2, 512], BF16)
        gsT = [A.alloc("gsT", [128, 4, 512], BF16) for _ in range(2)]
        vtok = [A.alloc("vtok", [128, 4, 512], BF16) for _ in range(2)]
        ktok = [A.alloc("ktok", [128, 4, 256], BF16) for _ in range(2)]
        yT = A.alloc("yT", [128, 4, 512], BF16)
        S32 = A.alloc("S32", [128, 2, 512], F32)
        Sb = [A.alloc("Sb", [128, 2, 512], BF16) for _ in range(2)]
        PT = A.alloc("PT", [128, 1280], BF16)
        osq = [A.alloc("osq", [128, 512], BF16) for _ in range(2)]
        of = A.alloc("of", [128, 512], F32)
        rs = A.alloc("rs", [128, 512], F32)
        t1 = A.alloc("t1", [128, 512], F32)
        t2 = A.alloc("t2", [128, 512], F32)
        wv = od_w_in.rearrange("(kc p) n -> p kc n", p=128)
        wov = od_w_out.rearrange("(fc p) n -> p fc n", p=128)

        dma("sp", cosT[:, :], c_rope1[0], r=[], w=["cosT"])
        dma("sp", sinT[:, :], c_rope1[1], r=[], w=["sinT"])
        dma("sp", kdec[:, :], c_kdec, r=[], w=["kdec"])

        def load_qk(h):
            dma("pool", Whqk[:, :, 0:256], wv[:, :, h * 256:(h + 1) * 256], r=[], w=["Whqk"])
            dma("pool", Whqk[:, :, 256:512], wv[:, :, 1024 + h * 256:1024 + (h + 1) * 256], r=[], w=["Whqk"])

        def load_g(h):
            dma("pool", Whg[:, :, :], wv[:, :, 4096 + h * 512:4096 + (h + 1) * 512], r=[], w=["Whg"])

        def load_v(h):
            dma("pool", Whv[:, :, :], wv[:, :, 2048 + h * 512:2048 + (h + 1) * 512], r=[], w=["Whv"])

        def load_o(h):
            dma("pool", Woh[:, :, :], wov[:, h * 4:(h + 1) * 4, :], r=[], w=["Woh"])

        def load_tabs(h):
            dma("sp", dmT[:, :], c_dm[h], r=[], w=["dmT"])
            dma("sp", qdec[:, :], c_qdec[h], r=[], w=["qdec"])

        load_qk(0)
        load_g(0)
        load_v(0)
        load_o(0)
        cnt = {"p": 0}

        def blocks_of(ti):
            if ti == 0:
                return [(0, 16)]
            return [(128 * j, 128) for j in range(4)]

        def front(h, ti, par):
            t0, n = TILES[ti]
            last = (ti == 4)
            for which, (dst, coff) in enumerate(((qT[par], 0), (kT[par], 256))):
                xb = []
                for dcq in range(2):
                    pb = dcq
                    xb.append(pb)
                    for kc in range(8):
                        mm(ps[pb][:, :n], Whqk[:, kc, coff + dcq * 128:coff + (dcq + 1) * 128],
                           hb[:, kc, t0:t0 + n], kc == 0, kc == 7, r=["Whqk", Hk(kc, ti)], w=[PK(pb)])
                if last and which == 1 and h + 1 < 4:
                    load_qk(h + 1)
                cs_, sn_ = cosT[:, t0:t0 + n], sinT[:, t0:t0 + n]
                x1, x2 = ps[xb[0]][:, :n], ps[xb[1]][:, :n]
                tt("dve", t1[:, :n], x1, cs_, ALU.mult, r=[PK(xb[0]), "cosT"], w=["t1"])
                tt("dve", t2[:, :n], x2, sn_, ALU.mult, r=[PK(xb[1]), "sinT"], w=["t2"])
                tt("dve", dst[:, 0, :n], t1[:, :n], t2[:, :n], ALU.subtract, r=["t1", "t2"],
                   w=[("qk", par, which, 0)])
                tt("dve", t1[:, :n], x1, sn_, ALU.mult, r=[PK(xb[0]), "sinT"], w=["t1"])
                tt("dve", t2[:, :n], x2, cs_, ALU.mult, r=[PK(xb[1]), "cosT"], w=["t2"])
                tt("dve", dst[:, 1, :n], t1[:, :n], t2[:, :n], ALU.add, r=["t1", "t2"],
                   w=[("qk", par, which, 1)])
                yield
            for fc in range(4):
                pb = fc % 2
                for kc in range(8):
                    mm(ps[pb][:, :n], Whg[:, kc, fc * 128:(fc + 1) * 128], hb[:, kc, t0:t0 + n],
                       kc == 0, kc == 7, r=["Whg", Hk(kc, ti)], w=[PK(pb)])
                act(gsT[par][:, fc, :n], ps[pb][:, :n], AF.Silu, r=[PK(pb)], w=[("gsT", par, fc)])
                if fc % 2 == 1:
                    yield
            if last and h + 1 < 4:
                load_g(h + 1)
            for b, (cs, cn) in enumerate(blocks_of(ti)):
                pb = b % 2
                for kc in range(8):
                    mm(ps[pb][:cn, :], hb[:, kc, t0 + cs:t0 + cs + cn], Whv[:, kc, :], kc == 0, kc == 7,
                       r=["Whv", Hk(kc, ti)], w=[PK(pb)])
                act(vtok[par][:cn, b, :], ps[pb][:cn, :], AF.Copy, r=[PK(pb)], w=[("vtok", par, b)])
                if b % 2 == 1:
                    yield
            if last and h + 1 < 4:
                load_v(h + 1)
            psb = ps[7][:, :].bitcast(BF16)
            for b, (cs, cn) in enumerate(blocks_of(ti)):
                for dcq in range(2):
                    tr(psb[:cn, b * 256 + dcq * 128:b * 256 + (dcq + 1) * 128], kT[par][:, dcq, cs:cs + cn],
                       identb[:, :], r=[("qk", par, 1, dcq), "identb"], w=[PK(7)])
            for b, (cs, cn) in enumerate(blocks_of(ti)):
                kcol = h * 5 + (4 if ti == 0 else b)
                act(ktok[par][:cn, b, :], psb[:cn, b * 256:(b + 1) * 256], AF.Identity, r=[PK(7), "kdec"],
                    w=[("ktok", par, b)], scale=kdec[:cn, kcol:kcol + 1])
            yield

        def back(h, ti, par, spar, first):
            t0, n = TILES[ti]
            blks = blocks_of(ti)
            nb = len(blks)
            for dcq in range(2):
                pbs = 5 + dcq
                for b, (cs, cn) in enumerate(blks):
                    mm(ps[pbs][:, :], ktok[par][:cn, b, dcq * 128:(dcq + 1) * 128], vtok[par][:cn, b, :],
                       b == 0, b == nb - 1, r=[("ktok", par, b), ("vtok", par, b)], w=[PK(pbs)])
                if first:
                    act(S32[:, dcq, :], ps[pbs][:, :], AF.Copy, r=[PK(pbs)], w=[("S32", dcq)])
                else:
                    stt(S32[:, dcq, :], S32[:, dcq, :], GAM[h] ** 512, ps[pbs][:, :], ALU.mult, ALU.add,
                        r=[("S32", dcq), PK(pbs)], w=[("S32", dcq)])
            act(Sb[1 - spar][:, :, :], S32[:, :, :], AF.Copy, r=[("S32", 0), ("S32", 1)], w=[("Sb", 1 - spar)])
            for dcq in range(2):
                tt("dve", qdT[:, dcq, :n], qT[par][:, dcq, :n], qdec[:, :n], ALU.mult,
                   r=[("qk", par, 0, dcq), "qdec"], w=[("qdT", dcq)])
            yield
            offs = []
            off = 0
            for jb, (cs, cn) in enumerate(blks):
                nn = n - cs
                sb_ = 2 + jb % 2
                for dcq in range(2):
                    mm(ps[sb_][:cn, :nn], kT[par][:, dcq, cs:cs + cn], qT[par][:, dcq, cs:n], dcq == 0, dcq == 1,
                       r=[("qk", par, 1, dcq), ("qk", par, 0, dcq)], w=[PK(sb_)])
                tt("dve", PT[:cn, off:off + nn], ps[sb_][:cn, :nn], dmT[:cn, :nn], ALU.mult, r=[PK(sb_), "dmT"],
                   w=[("PT", jb)])
                offs.append((off, nn))
                off += nn
                if jb % 2 == 1:
                    yield
            yield
            for fc in range(4):
                ob = (4, 2)[fc % 2]
                started = False
                if not first:
                    for dcq in range(2):
                        mm(ps[ob][:, :n], Sb[spar][:, dcq, fc * 128:(fc + 1) * 128], qdT[:, dcq, :n],
                           dcq == 0, False, r=[("Sb", spar), ("qdT", dcq)], w=[PK(ob)])
                    started = True
                for jb, (cs, cn) in enumerate(blks):
                    o_, nn = offs[jb]
                    mm(ps[ob][:, cs:n], vtok[par][:cn, jb, fc * 128:(fc + 1) * 128], PT[:cn, o_:o_ + nn],
                       (not started) and jb == 0, jb == nb - 1, r=[("vtok", par, jb), ("PT", jb)], w=[PK(ob)])
                act(yT[:, fc, :n], ps[ob][:, :n], AF.Copy, r=[PK(ob)], w=[("yT", fc)])
                act(osq[fc % 2][:, :n], ps[ob][:, :n], AF.Square, r=[PK(ob)], w=[("osq", fc % 2)])
                mm(ps[3][:, :n], onesb[:, :], osq[fc % 2][:, :n], fc == 0, fc == 3, r=[("osq", fc % 2), "onesb"],
                   w=[PK(3)])
                yield
            act(rs[:, :n], ps[3][:, :n], AF.Sqrt, r=[PK(3), "epst"], w=["rs"], bias=epst[:, :], scale=1.0 / 512)
            recip(rs[:, :n], rs[:, :n], r=["rs"], w=["rs"])
            for fc in range(4):
                tt("dve", of[:, :n], yT[:, fc, :n], rs[:, :n], ALU.mult, r=[("yT", fc), "rs"], w=["of"])
                tt("dve", yT[:, fc, :n], of[:, :n], gsT[par][:, fc, :n], ALU.mult, r=["of", ("gsT", par, fc)],
                   w=[("yT", fc)])
            yield
            for dc in range(8):
                pb = 5 + dc % 2
                for fc in range(4):
                    mm(ps[pb][:, :n], Woh[:, fc, dc * 128:(dc + 1) * 128], yT[:, fc, :n], fc == 0, fc == 3,
                       r=["Woh", ("yT", fc)], w=[PK(pb)])
                tt("dve", R[:, dc, t0:t0 + n], R[:, dc, t0:t0 + n], ps[pb][:, :n], ALU.add,
                   r=[Rk(dc, ti), PK(pb)], w=[Rk(dc, ti)])
                if dc % 2 == 1:
                    yield
            if ti == 4 and h + 1 < 4:
                load_o(h + 1)

        def interleave(g1, g2):
            gens = [g for g in (g1, g2) if g is not None]
            while gens:
                for g in list(gens):
                    try:
                        next(g)
                    except StopIteration:
                        gens.remove(g)

        items = [(h, ti) for h in range(4) for ti in range(5)]
        interleave(front(0, 0, 0), None)
        for k, (h, ti) in enumerate(items):
            par = k % 2
            if ti == 0:
                load_tabs(h)
            nxt = front(items[k + 1][0], items[k + 1][1], 1 - par) if k + 1 < len(items) else None
            spar = ti % 2
            interleave(back(h, ti, par, spar, ti == 0), nxt)
        P.barrier()

    def phase_lru(loadg=None):
        A.reset()
        A.top = SB_TOP - 4160
        Wg = [A.alloc("Wg", [128, 8, 128], BF16) for _ in range(4)]
        Wr = [A.alloc("Wr", [128, 8, 128], BF16) for _ in range(4)]
        Wa = A.alloc("Wa", [128, 4, 128], BF16)
        Wx = A.alloc("Wx", [128, 4, 128], BF16)
        Wo = A.alloc("Wo", [128, 4, 1024], BF16)
        cneg = A.alloc("cneg", [128, 4], F32)
        yrT = A.alloc("yrT", [128, 4, T], BF16)
        prec_ = [A.alloc("prec", [128, T + 3], F32) for _ in range(2)]
        names = ("xc", "rg", "ig", "aa", "mu", "bb", "xg", "ug", "hs0", "hs1")
        tmp = [{nm: A.alloc(nm, [128, 512], F32) for nm in names} for _ in range(2)]
        xcb_ = [A.alloc("xcb", [128, 512], BF16) for _ in range(2)]
        wiv = ev_w_in.rearrange("(kc p) n -> p kc n", p=128)
        for c in range(4):
            dma("pool", Wr[c][:, :, :], wiv[:, :, 512 + c * 128:512 + (c + 1) * 128], r=[], w=[("Wr", c)])
            dma("pool", Wg[c][:, :, :], wiv[:, :, c * 128:(c + 1) * 128], r=[], w=[("Wg", c)])
            if c == 1:
                dma("pool", Wa[:, :, :], ev_w_rg_a.rearrange("h i j -> i h j"), r=[], w=["Wa"])
                dma("pool", Wx[:, :, :], ev_w_rg_x.rearrange("h i j -> i h j"), r=[], w=["Wx"])
        dma("pool", Wo[:, :, :], ev_w_out[0:512, :].rearrange("(c p) n -> p c n", p=128), r=[], w=["Wo"])
        act(cneg[:, :], small[:, 28:32], AF.Exp, r=["small"], w=["cneg"], scale=-1.0)
        act(cneg[:, :], cneg[:, :], AF.Ln, r=["cneg"], w=["cneg"], bias=1.0)
        act(cneg[:, :], cneg[:, :], AF.Copy, r=["cneg"], w=["cneg"], scale=-8.0)
        hprm = A.alloc("hprm", [128, 12], F32)
        act(hprm[:, 0:8], small[:, 20:28], AF.Copy, r=["small"], w=["hprm"], scale=0.5)
        act(hprm[:, 8:12], cneg[:, :], AF.Copy, r=["cneg", "hprm"], w=["hprm"], scale=0.5)
        for q_ in range(2):
            P.op("dve", lambda e, q_=q_: e.memset(prec_[q_][:, 0:3], 0.0), r=[], w=[("prec_h", q_)])

        def stream(q_, chunks):
            tm = tmp[q_]
            xc, rg, ig, aa, mu, bb, xg, ug = (tm[k_] for k_ in ("xc", "rg", "ig", "aa", "mu", "bb", "xg", "ug"))
            xcb = xcb_[q_]
            prec = prec_[q_]
            K_ = lambda nm: (nm, q_)
            pbase = 4 * q_
            pcnt = 0
            hcnt = 0
            for c in chunks:
                prev = None
                for ti, (t0, n) in enumerate(TILES):
                    pb = pbase + pcnt % 2
                    pcnt += 1
                    for kc in range(8):
                        mm(ps[pb][:, :n], Wr[c][:, kc, :], hb[:, kc, t0:t0 + n], kc == 0, kc == 7,
                           r=[("Wr", c), Hk(kc, ti)], w=[PK(pb)])
                    act(prec[:, 3 + t0:3 + t0 + n], ps[pb][:, :n], AF.Copy, r=[PK(pb)], w=[("prec", q_, ti)])
                    pr = [("prec", q_, ti), ("prec_h", q_)] + ([("prec", q_, ti - 1)] if ti > 0 else [])
                    act(xc[:, :n], prec[:, 3 + t0:3 + t0 + n], AF.Identity, r=pr, w=[K_("xc")],
                        bias=small[:, 16 + c:17 + c], scale=small[:, 12 + c:13 + c])
                    for j in range(3):
                        stt(xc[:, :n], prec[:, t0 + j:t0 + j + n], small[:, 4 * j + c:4 * j + c + 1], xc[:, :n],
                            ALU.mult, ALU.add, r=pr + [K_("xc")], w=[K_("xc")])
                    act(xcb[:, :n], xc[:, :n], AF.Copy, r=[K_("xc")], w=[K_("xcb")])
                    yield
                    g0, g1 = pbase + 2, pbase + 3
                    mm(ps[g0][:, :n], Wa[:, c, :], xcb[:, :n], True, True, r=["Wa", K_("xcb")], w=[PK(g0)])
                    mm(ps[g1][:, :n], Wx[:, c, :], xcb[:, :n], True, True, r=["Wx", K_("xcb")], w=[PK(g1)])
                    act(rg[:, :n], ps[g0][:, :n], AF.Tanh, r=[PK(g0), "hprm"], w=[K_("rg")],
                        bias=hprm[:, c:c + 1], scale=0.5)
                    act(ig[:, :n], ps[g1][:, :n], AF.Tanh, r=[PK(g1), "hprm"], w=[K_("ig")],
                        bias=hprm[:, 4 + c:5 + c], scale=0.5)
                    act(aa[:, :n], rg[:, :n], AF.Exp, r=[K_("rg"), "hprm"], w=[K_("aa")],
                        scale=hprm[:, 8 + c:9 + c], bias=hprm[:, 8 + c:9 + c])
                    yield
                    pb = pbase + pcnt % 2
                    pcnt += 1
                    for kc in range(8):
                        mm(ps[pb][:, :n], Wg[c][:, kc, :], hb[:, kc, t0:t0 + n], kc == 0, kc == 7,
                           r=[("Wg", c), Hk(kc, ti)], w=[PK(pb)])
                    act(xg[:, :n], ps[pb][:, :n], AF.Copy, r=[PK(pb)], w=[K_("xg")], scale=0.5)
                    act(ug[:, :n], ps[pb][:, :n], AF.Square, r=[PK(pb)], w=[K_("ug")])
                    tt("dve", mu[:, :n], aa[:, :n], aa[:, :n], ALU.mult, r=[K_("aa")], w=[K_("mu")])
                    act(mu[:, :n], mu[:, :n], AF.Sqrt, r=[K_("mu")], w=[K_("mu")], bias=0.25, scale=-0.25)
                    stt(bb[:, :n], ig[:, :n], 1.0, xc[:, :n], ALU.add, ALU.mult, r=[K_("ig"), K_("xc")],
                        w=[K_("bb")])
                    yield
                    tt("dve", bb[:, :n], bb[:, :n], mu[:, :n], ALU.mult, r=[K_("bb"), K_("mu")], w=[K_("bb")])
                    hc = tm["hs%d" % (hcnt % 2)]
                    hk = ("hs", q_, hcnt % 2)
                    init = 0.0 if prev is None else prev[0]
                    P.op("dve", lambda e, o=hc[:, :n], a_=aa[:, :n], b_=bb[:, :n], i_=init:
                         e.tensor_tensor_scan(out=o, data0=a_, data1=b_, initial=i_, op0=ALU.mult, op1=ALU.add),
                         r=[K_("aa"), K_("bb")] + ([prev[1]] if prev is not None else []), w=[hk])
                    prev = (hc[:, n - 1:n], hk)
                    hcnt += 1
                    ts("dve", ug[:, :n], ug[:, :n], 0.044715, 1.0, ALU.mult, ALU.add, r=[K_("ug")], w=[K_("ug")])
                    tt("dve", ug[:, :n], ug[:, :n], xg[:, :n], ALU.mult, r=[K_("ug"), K_("xg")], w=[K_("ug")])
                    act(ug[:, :n], ug[:, :n], AF.Tanh, r=[K_("ug")], w=[K_("ug")],
                        scale=2.0 * math.sqrt(2.0 / math.pi))
                    yield
                    stt(ug[:, :n], ug[:, :n], 1.0, xg[:, :n], ALU.add, ALU.mult, r=[K_("ug"), K_("xg")],
                        w=[K_("ug")])
                    tt("dve", yrT[:, c, t0:t0 + n], ug[:, :n], hc[:, :n], ALU.mult, r=[K_("ug"), hk],
                       w=[("yrT", c, ti)])
                    yield

        g0_, g1_ = stream(0, (0, 2)), stream(1, (1, 3))
        live = [g0_, g1_]
        if loadg is not None:
            for _ in range(5):
                step(loadg)
            live.append(loadg)
        while live:
            for g_ in list(live):
                if not step(g_):
                    live.remove(g_)
        pcnt = 0
        for ti, (t0, n) in enumerate(TILES):
            for dc in range(8):
                pb = pcnt % 8
                pcnt += 1
                for c in range(4):
                    mm(ps[pb][:, :n], Wo[:, c, dc * 128:(dc + 1) * 128], yrT[:, c, t0:t0 + n], c == 0, c == 3,
                       r=["Wo", ("yrT", c, ti)], w=[PK(pb)])
                tt("dve", R[:, dc, t0:t0 + n], R[:, dc, t0:t0 + n], ps[pb][:, :n], ALU.add,
                   r=[Rk(dc, ti), PK(pb)], w=[Rk(dc, ti)])
        A.top = SB_TOP
        P.barrier()

    def phase_mla():
        A.reset()
        Wq = A.alloc("Wq", [128, 8, 416], BF16)
        Wsw = A.alloc("Wsw", [128, 8, 96], BF16)
        Wuq = A.alloc("Wuq", [128, 2, 8, 96], BF16)
        Wuqs = A.alloc("Wuqs", [128, 2, 8, 96], BF16)
        Wukv = A.alloc("Wukv", [128, 1024], BF16)
        Wo = A.alloc("Wo", [128, 4, 1024], BF16)
        cos0 = A.alloc("cos0", [128, T], F32)
        sin0 = A.alloc("sin0", [128, T], F32)
        qnT = A.alloc("qnT", [128, 2, T], BF16)
        kvnT = A.alloc("kvnT", [128, T], BF16)
        kpe = A.alloc("kpe", [128, T], BF16)
        yaT = A.alloc("yaT", [128, 4, T], BF16)
        KT_ = [A.alloc("KT", [128, T], BF16) for _ in range(2)]
        Vh_ = [A.alloc("Vh", [128, 17, 128], BF16) for _ in range(2)]
        onesf = A.alloc("onesf", [128, 128], F32)
        zb = A.alloc("zb", [128, 128], BF16)
        WARM = 0
        t1 = A.alloc("t1", [128, 512], F32)
        t2 = A.alloc("t2", [128, 512], F32)
        mark_pq = A.cur
        pq32 = A.alloc("pq32", [128, 3, 512], F32)
        sq = A.alloc("sq", [128, 3, 512], BF16)
        rs = A.alloc("rs", [128, 512], F32)
        wiv = ev_w_in.rearrange("(kc p) n -> p kc n", p=128)
        SC = 96.0 ** -0.5
        dma("pool", Wq[:, :, :], wiv[:, :, 1024:1440], r=[], w=["Wq"])
        dma("pool", Wuq[:, :, :, :], ev_w_uq.rearrange("(kc p) (h e) -> p kc h e", p=128, e=96), r=[], w=["Wuq"])
        dma("pool", Wukv[:, :], ev_w_ukv, r=[], w=["Wukv"])
        dma("pool", Wo[:, :, :], ev_w_out[512:1024, :].rearrange("(c p) n -> p c n", p=128), r=[], w=["Wo"])
        dma("sp", cos0[64:96, :], c_rope0[0], r=[], w=["cos0"])
        dma("sp", sin0[64:96, :], c_rope0[1], r=[], w=["sin0"])
        P.op("dve", lambda e: e.memset(Wsw[:, :, :], 0.0), r=[], w=["Wsw"])
        P.op("dve", lambda e: e.memset(Wuqs[:, :, :, :], 0.0), r=[], w=["Wuqs"])
        act(Wsw[:, :, 64:80], Wq[:, :, 400:416], AF.Copy, r=["Wq", "Wsw"], w=["Wsw"], scale=-1.0)
        act(Wsw[:, :, 80:96], Wq[:, :, 384:400], AF.Copy, r=["Wq", "Wsw"], w=["Wsw"])
        for kc in range(2):
            act(Wuqs[:, kc, :, 64:80], Wuq[:, kc, :, 80:96], AF.Copy, r=["Wuq", "Wuqs"], w=["Wuqs"], scale=-1.0)
            act(Wuqs[:, kc, :, 80:96], Wuq[:, kc, :, 64:80], AF.Copy, r=["Wuq", "Wuqs"], w=["Wuqs"])
        P.op("dve", lambda e: e.memset(Vh_[0][:, :, 64:65], 1.0), r=[], w=["Vh1"])
        P.op("dve", lambda e: e.memset(Vh_[1][:, :, 0:64], 0.0), r=[], w=["Vh1"])
        P.op("dve", lambda e: e.memset(Vh_[1][:, :, 0:1], 1.0), r=["Vh1"], w=["Vh1"])
        P.op("dve", lambda e: e.memset(onesf[:, :], 1.0), r=[], w=["onesf"])
        P.op("dve", lambda e: e.memset(zb[:, :], 0.0), r=[], w=["zb"])
        pcnt = 0
        for ti, (t0, n) in enumerate(TILES):
            for c in range(3):
                pb = pcnt % 2
                pcnt += 1
                for kc in range(8):
                    mm(ps[pb][:, :n], Wq[:, kc, c * 128:(c + 1) * 128], hb[:, kc, t0:t0 + n], kc == 0, kc == 7,
                       r=["Wq", Hk(kc, ti)], w=[PK(pb)])
                act(pq32[:, c, :n], ps[pb][:, :n], AF.Copy, r=[PK(pb)], w=[("pq32", c)])
                act(sq[:, c, :n], ps[pb][:, :n], AF.Square, r=[PK(pb)], w=[("sq", c)])
            for grp, (c0, c1, dd) in enumerate(((0, 2, 256.0), (2, 3, 128.0))):
                pb = 2 + grp
                for c in range(c0, c1):
                    mm(ps[pb][:, :n], onesb[:, :], sq[:, c, :n], c == c0, c == c1 - 1, r=[("sq", c), "onesb"],
                       w=[PK(pb)])
                act(rs[:, :n], ps[pb][:, :n], AF.Sqrt, r=[PK(pb), "epst"], w=["rs"], bias=epst[:, :], scale=1.0 / dd)
                recip(rs[:, :n], rs[:, :n], r=["rs"], w=["rs"])
                for c in range(c0, c1):
                    dst = qnT[:, c, t0:t0 + n] if c < 2 else kvnT[:, t0:t0 + n]
                    stt(dst, pq32[:, c, :n], small[:, 32 + c:33 + c], rs[:, :n], ALU.mult, ALU.mult,
                        r=[("pq32", c), "rs"], w=[("lat", c, ti)])
            for kc in range(8):
                mm(ps[4][0:96, :n], Wq[:, kc, 320:416], hb[:, kc, t0:t0 + n], kc == 0, kc == 7, r=["Wq", Hk(kc, ti)],
                   w=[PK(4)])
            for kc in range(8):
                mm(ps[5][0:96, :n], Wsw[:, kc, :], hb[:, kc, t0:t0 + n], kc == 0, kc == 7, r=["Wsw", Hk(kc, ti)],
                   w=[PK(5)])
            tt("dve", t1[64:96, :n], ps[4][64:96, :n], cos0[64:96, t0:t0 + n], ALU.mult, r=[PK(4), "cos0"], w=["t1"])
            tt("dve", t2[64:96, :n], ps[5][64:96, :n], sin0[64:96, t0:t0 + n], ALU.mult, r=[PK(5), "sin0"], w=["t2"])
            tt("dve", kpe[64:96, t0:t0 + n], t1[64:96, :n], t2[64:96, :n], ALU.add, r=["t1", "t2"], w=[("kpe", ti)])
        P.barrier()
        A.cur = mark_pq
        QT = [A.alloc("QT", [128, 512], BF16) for _ in range(2)]
        PTs = [A.alloc("PTs", [128, 512], BF16) for _ in range(4)]
        recr = A.alloc("recr", [128, 512], F32)
        rhb = A.alloc("rhb", [128, 512], BF16)
        rlb = A.alloc("rlb", [128, 512], BF16)
        bcs = A.alloc("bcs", [128, 512], F32)
        qcnt = 0
        ptc = 0
        scnt = 0
        ocnt = [0]
        norm_pending = []
        for h in range(8):
            hp = h % 2
            KT, Vh = KT_[hp], Vh_[hp]
            vo = 0 if hp == 0 else 64
            srow = 64 if hp == 0 else 0
            MV = 65 if hp == 0 else 128
            fcy = h // 2
            act(KT[64:96, :], kpe[64:96, :], AF.Copy, r=[("kpe", ti) for ti in range(5)], w=[("KTpe", hp)])
            for ti, (t0, n) in enumerate(TILES):
                pb = pcnt % 2
                pcnt += 1
                mm(ps[pb][0:64, :n], Wukv[:, h * 128:h * 128 + 64], kvnT[:, t0:t0 + n], True, True,
                   r=["Wukv", ("lat", 2, ti)], w=[PK(pb)])
                act(KT[0:64, t0:t0 + n], ps[pb][0:64, :n], AF.Copy, r=[PK(pb)], w=[("KT", hp, ti)])
            for grp in range(3):
                blks = [0] if grp == 0 else list(range(1 + 8 * (grp - 1), 1 + 8 * grp))
                pb = pcnt % 2
                pcnt += 1
                for i, bi in enumerate(blks):
                    b0, bn = BLOCKS[bi]
                    mm(ps[pb][:bn, i * 64:(i + 1) * 64], kvnT[:, b0:b0 + bn], Wukv[:, h * 128 + 64:h * 128 + 128],
                       True, True, r=["Wukv", ("lat", 2, tile_of_block(bi))], w=[PK(pb)])
                if grp == 0:
                    act(Vh[:16, 0, vo:vo + 64], ps[pb][:16, 0:64], AF.Copy, r=[PK(pb), "Vh1"], w=[("Vh", hp, 0)])
                else:
                    act(Vh[:, blks[0]:blks[0] + 8, vo:vo + 64],
                        ps[pb][:, :].rearrange("p (b e) -> p b e", b=8), AF.Copy, r=[PK(pb), "Vh1"],
                        w=[("Vh", hp, grp)])
            vkeys = [("Vh", hp, 0), ("Vh", hp, 1), ("Vh", hp, 2)]
            def q_prep(ti):
                nonlocal qcnt
                t0, n = TILES[ti]
                qt = QT[qcnt % 2]
                qk = ("QT", qcnt % 2)
                qcnt += 1
                for kc in range(2):
                    mm(ps[2][0:96, :n], Wuq[:, kc, h, :], qnT[:, kc, t0:t0 + n], kc == 0, kc == 1,
                       r=["Wuq", ("lat", kc, ti)], w=[PK(2)])
                for kc in range(2):
                    mm(ps[3][0:96, :n], Wuqs[:, kc, h, :], qnT[:, kc, t0:t0 + n], kc == 0, kc == 1,
                       r=["Wuqs", ("lat", kc, ti)], w=[PK(3)])
                act(qt[0:64, :n], ps[2][0:64, :n], AF.Copy, r=[PK(2)], w=[qk])
                tt("dve", t1[64:96, :n], ps[2][64:96, :n], cos0[64:96, t0:t0 + n], ALU.mult, r=[PK(2), "cos0"],
                   w=["t1"])
                tt("dve", t2[64:96, :n], ps[3][64:96, :n], sin0[64:96, t0:t0 + n], ALU.mult, r=[PK(3), "sin0"],
                   w=["t2"])
                tt("dve", qt[64:96, :n], t1[64:96, :n], t2[64:96, :n], ALU.add, r=["t1", "t2"], w=[qk])
                return qt, qk

            nxt_q = q_prep(0)
            for ti, (t0, n) in enumerate(TILES):
                qt, qk = nxt_q
                qblocks = [bi for bi in range(17) if tile_of_block(bi) == ti]
                ob = 6 + ocnt[0] % 2
                ocnt[0] += 1
                O = ps[ob]
                kbs = list(range(0, qblocks[-1] + 1))

                def scores(kb):
                    nonlocal scnt, ptc
                    ks, kn = BLOCKS[kb]
                    c0 = max(ks, t0) - t0
                    sb = 4 + scnt % 2
                    scnt += 1
                    mm(ps[sb][:kn, c0:n], KT[0:96, ks:ks + kn], qt[0:96, c0:n], True, True,
                       r=[("KTpe", hp), ("KT", hp, tile_of_block(kb)), qk], w=[PK(sb)])
                    pt = PTs[ptc % 4]
                    pk = ("PTs", ptc % 4)
                    ptc += 1
                    act(pt[:kn, c0:n], ps[sb][:kn, c0:n], AF.Exp, r=[PK(sb)], w=[pk], scale=SC)
                    if ks >= t0:
                        tt("dve", pt[:kn, c0:c0 + kn], pt[:kn, c0:c0 + kn], maskb[:kn, :kn], ALU.mult,
                           r=[pk, "maskb"], w=[pk])
                    return pt, pk

                def pv(kb, pt, pk):
                    ks, kn = BLOCKS[kb]
                    c0 = max(ks, t0) - t0
                    mm(O[0:MV, c0:n], Vh[:kn, kb, 0:MV], pt[:kn, c0:n], kb == 0, kb == kbs[-1],
                       r=[pk] + vkeys, w=[PK(ob)])
                    if WARM and kb != kbs[-1] and n == 512:
                        for _ in range(WARM):
                            mm(O[0:MV, 0:n], zb[:, 0:MV], kvnT[:, 0:n], False, False, r=["zb"], w=[PK(ob)])

                pend = []
                for ki, kb in enumerate(kbs):
                    pend.append((kb,) + scores(kb))
                    if len(pend) > 1:
                        pv(*pend.pop(0))
                    if ki == 2 and norm_pending:
                        norm_pending.pop(0)()
                while pend:
                    pv(*pend.pop(0))
                if norm_pending:
                    norm_pending.pop(0)()
                if ti + 1 < 5:
                    nxt_q = q_prep(ti + 1)

                def normalize(ob=ob, O=O, n=n, t0=t0, ti=ti, vo=vo, srow=srow, fcy=fcy, hp=hp):
                    nonlocal pcnt
                    recip(recr[srow:srow + 1, :n], O[srow:srow + 1, :n], r=[PK(ob)], w=["recr"])
                    P.op("dve", lambda e: e.tensor_copy(out=rhb[srow:srow + 1, :n], in_=recr[srow:srow + 1, :n]),
                         r=["recr"], w=["rhb"])
                    tt("dve", rlb[srow:srow + 1, :n], recr[srow:srow + 1, :n], rhb[srow:srow + 1, :n], ALU.subtract,
                       r=["recr", "rhb"], w=["rlb"])
                    bb_ = pcnt % 2
                    pcnt += 1
                    mm(ps[bb_][0:vo + 64, :n], onesb[srow:srow + 1, 0:vo + 64], rhb[srow:srow + 1, :n], True, False,
                       r=["onesb", "rhb"], w=[PK(bb_)])
                    mm(ps[bb_][0:vo + 64, :n], onesb[srow:srow + 1, 0:vo + 64], rlb[srow:srow + 1, :n], False, True,
                       r=["onesb", "rlb"], w=[PK(bb_)])
                    act(bcs[vo:vo + 64, :n], ps[bb_][vo:vo + 64, :n], AF.Copy, r=[PK(bb_)], w=["bcs"])
                    tt("dve", yaT[vo:vo + 64, fcy, t0:t0 + n], O[vo:vo + 64, :n], bcs[vo:vo + 64, :n], ALU.mult,
                       r=[PK(ob), "bcs"], w=[("yaT", fcy, ti, hp)])

                norm_pending.append(normalize)
        while norm_pending:
            norm_pending.pop(0)()
        for ti, (t0, n) in enumerate(TILES):
            for dc in range(8):
                pb = pcnt % 6
                pcnt += 1
                for fc in range(4):
                    mm(ps[pb][:, :n], Wo[:, fc, dc * 128:(dc + 1) * 128], yaT[:, fc, t0:t0 + n], fc == 0, fc == 3,
                       r=["Wo", ("yaT", fc, ti, 0), ("yaT", fc, ti, 1)], w=[PK(pb)])
                tt("dve", R[:, dc, t0:t0 + n], R[:, dc, t0:t0 + n], ps[pb][:, :n], ALU.add,
                   r=[Rk(dc, ti), PK(pb)], w=[Rk(dc, ti)])
        P.barrier()

    for s in range(1 if 'oneseq' in KVAR else SEQ_PER_CORE):
        if stage in ("full", "l0", "lru"):
            phase_lru(load_gen(s))
        elif 'noload' not in KVAR:
            phase_load(s)
        if stage in ("full", "l0", "mla"):
            phase_mla()
        if stage in ("full", "l0"):
            seq_ln_mlp_ln(0, 0, 1, False)
        if stage in ("full", "ret"):
            phase_ret()
        if stage in ("full",):
            seq_ln_mlp_ln(1, 2, 3, True)
        if stage in ("mlp",):
            phase_mlp(1, end_barrier=False)
        if stage in ("mlp", "ret", "lru", "mla", "l0"):
            phase_ln(3, final=True)
        if 'nostore' not in KVAR:
            phase_store(s)
    P.emit()
    nc._in_names = in_names
    return nc, in_names


def make_consts():
    c = {}
    c["c_ident"] = np.eye(128, dtype=np.float32)
    k = np.arange(128)
    c["c_mask"] = (k[None, :] >= k[:, None]).astype(np.float32)
    pos = np.arange(T, dtype=np.float32)
    inv0 = (10000.0 ** (-np.arange(16, dtype=np.float32) / 16)).astype(np.float32)
    ang0 = pos[None, :] * inv0[np.arange(32) % 16][:, None]
    c["c_rope0"] = np.stack([np.cos(ang0), np.sin(ang0)]).astype(np.float32)
    inv1 = (10000.0 ** (-np.arange(128, dtype=np.float32) / 128)).astype(np.float32)
    ang1 = pos[None, :] * inv1[:, None]
    c["c_rope1"] = np.stack([np.cos(ang1), np.sin(ang1)]).astype(np.float32)
    gam = np.array([1.0 - 2.0 ** (-5.0 - h) for h in range(4)], dtype=np.float64)
    m = np.arange(512)
    diff = (m[None, :] - k[:, None]).astype(np.float64)
    dm = np.where(diff >= 0, gam[:, None, None] ** np.maximum(diff, 0.0)[None], 0.0) / 16.0
    c["c_dm"] = dm.astype(np.float32)
    i512 = (np.arange(512) + 1).astype(np.float64)
    c["c_qdec"] = np.broadcast_to((gam[:, None] ** i512[None, :])[:, None, :], (4, 128, 512)).astype(np.float32).copy()
    kd = np.zeros((128, 20), dtype=np.float64)
    for h in range(4):
        for b_ in range(4):
            kd[:, h * 5 + b_] = gam[h] ** (511.0 - (128.0 * b_ + k)) / 16.0
        kd[:, h * 5 + 4] = gam[h] ** np.maximum(15.0 - k, 0.0) / 16.0
    c["c_kdec"] = kd.astype(np.float32)
    return c


def pack_small(inputs):
    g = lambda k: np.asarray(inputs[k], dtype=np.float32)
    cols = []
    cw = g("ev_conv_w")[0].reshape(4, 4, 128)
    cols.append(cw.transpose(2, 0, 1).reshape(128, 16))
    for k in ("ev_conv_b", "ev_b_rg_a", "ev_b_rg_x", "ev_lru_lambda"):
        cols.append(g(k)[0].reshape(4, 128).T)
    cols.append(g("ev_q_norm_g")[0].reshape(2, 128).T)
    cols.append(g("ev_kv_norm_g")[0].reshape(1, 128).T)
    for kind in ("g", "b"):
        for li in range(4):
            src = g(("ln_mix_" if li % 2 == 0 else "ln_mlp_") + kind)[li // 2]
            cols.append(src.reshape(8, 128).T)
    return np.ascontiguousarray(np.concatenate(cols, axis=1), dtype=np.float32)


_NC_CACHE = {}


def run(inputs, stage="full", trace=False):
    if stage not in _NC_CACHE:
        _NC_CACHE[stage] = build(stage)
    nc, in_names = _NC_CACHE[stage]
    consts = make_consts()
    f = lambda a: np.ascontiguousarray(np.asarray(a, dtype=np.float32))
    shared = {
        "meta_tokens": f(inputs["meta_tokens"]),
        "ev_w_in": f(inputs["ev_w_in"][0]), "ev_w_rg_a": f(inputs["ev_w_rg_a"][0]), "ev_w_rg_x": f(inputs["ev_w_rg_x"][0]), "ev_w_uq": f(inputs["ev_w_uq"][0]), "ev_w_ukv": f(inputs["ev_w_ukv"][0]),
        "ev_w_out": f(inputs["ev_w_out"][0]), "od_w_in": f(inputs["od_w_in"][0]),
        "od_w_out": f(inputs["od_w_out"][0]),
        "smallp": pack_small(inputs),
        "mlp_w1": f(inputs["mlp_w1"]), "mlp_w2": f(inputs["mlp_w2"]),
    }
    shared.update(consts)
    x = f(inputs["x"])
    in_maps = []
    for c in range(NCORES):
        m = {k: v for k, v in shared.items() if k in in_names}
        m["x"] = x[c * SEQ_PER_CORE:(c + 1) * SEQ_PER_CORE]
        in_maps.append(m)
    res = run_bass_kernel_spmd(nc, in_maps, core_ids=list(range(NCORES)), trace=trace)
    out = np.concatenate([np.asarray(r["out"]) for r in res.results], axis=0)
    return out.astype(np.float32), res


def kernel(**inputs):
    out, _ = run(inputs, "full")
    return out
```
